# Optimizing a Trainium2 kernel written in Bass

```python
import math
import jax, jax.numpy as jnp
from jax import lax
import numpy as np

D_MODEL = 1024
BATCH = 8
SEQ = 2048
DEPTH = 1

N_MEM = 256
GDN_HEADS = 8
GDN_DK = 128
GDN_DV = 128
GDN_KW = GDN_HEADS * GDN_DK
GDN_VW = GDN_HEADS * GDN_DV
SHORT_CONV = 4
CHUNK = 64
CONV_CH = D_MODEL
CONV_WIDTH = 31
XATTN_HEADS = 4
XATTN_HD = D_MODEL // XATTN_HEADS
D_FF = 2816
QKV_W = 2 * GDN_KW + GDN_VW
IN_SIZES = (QKV_W, GDN_VW, GDN_HEADS, GDN_HEADS, 2 * CONV_CH, 2 * D_MODEL)
IN_WIDTH = QKV_W + GDN_VW + 2 * GDN_HEADS + 2 * CONV_CH + 2 * D_MODEL
DN_ALPHA = (2 * DEPTH) ** 0.25
DN_BETA = (8 * DEPTH) ** -0.25
LN_EPS = 1e-5
RMS_EPS = 1e-6

kernel_name = "hybrid_gdn_conformer_deepnorm_block"


def layer_norm(x, g, b):
    xf = x.astype(jnp.float32)
    mu = jnp.mean(xf, axis=-1, keepdims=True)
    var = jnp.mean(jnp.square(xf - mu), axis=-1, keepdims=True)
    y = (xf - mu) * lax.rsqrt(var + LN_EPS)
    return (y * g.astype(jnp.float32) + b.astype(jnp.float32)).astype(x.dtype)


def rms_norm(x, g):
    xf = x.astype(jnp.float32)
    y = xf * lax.rsqrt(jnp.mean(jnp.square(xf), axis=-1, keepdims=True) + RMS_EPS)
    return (y * g.astype(jnp.float32)).astype(x.dtype)


def l2_normalize(x):
    return x * lax.rsqrt(jnp.sum(jnp.square(x), axis=-1, keepdims=True) + RMS_EPS)


def causal_depthwise_conv(x, w):
    k = w.shape[0]
    return lax.conv_general_dilated(
        x, w[:, None, :].astype(x.dtype), window_strides=(1,), padding=[(k - 1, 0)],
        dimension_numbers=("NWC", "WIO", "NWC"), feature_group_count=x.shape[-1])


def swiglu_ffn(x, w_gate, w_up, w_down):
    return (jax.nn.silu(x @ w_gate) * (x @ w_up)) @ w_down


def chunk_gated_delta_rule(q, k, v, g, beta):
    b, l, h, dk = q.shape
    dv = v.shape[-1]
    n = l // CHUNK
    q = l2_normalize(q) * (dk ** -0.5)
    k = l2_normalize(k)

    def chunks(t):
        return t.reshape(b, n, CHUNK, h, -1).transpose(0, 3, 1, 2, 4)

    qc, kc, vc = chunks(q), chunks(k), chunks(v)
    gc = jnp.cumsum(g.reshape(b, n, CHUNK, h).transpose(0, 3, 1, 2), axis=-1)
    bc = beta.reshape(b, n, CHUNK, h).transpose(0, 3, 1, 2)
    kb = kc * bc[..., None]
    vb = vc * bc[..., None]

    tri = jnp.tril(jnp.ones((CHUNK, CHUNK), dtype=bool))
    strict = jnp.tril(jnp.ones((CHUNK, CHUNK), dtype=bool), -1)
    diff = gc[..., :, None] - gc[..., None, :]
    decay = jnp.where(tri, jnp.exp(jnp.where(tri, diff, 0.0)), 0.0)

    m = jnp.where(strict, jnp.einsum("bhncd,bhnmd->bhncm", kb, kc) * decay, 0.0)
    t_sys = jnp.eye(CHUNK, dtype=q.dtype) + m
    u = lax.linalg.triangular_solve(t_sys, vb, left_side=True, lower=True, unit_diagonal=True)
    w = lax.linalg.triangular_solve(t_sys, kb * jnp.exp(gc)[..., None], left_side=True,
                                    lower=True, unit_diagonal=True)
    a_qk = jnp.einsum("bhncd,bhnmd->bhncm", qc, kc) * decay

    def step(state, xs):
        q_i, k_i, u_i, w_i, a_i, g_i = xs
        v_new = u_i - jnp.einsum("bhcd,bhde->bhce", w_i, state)
        o_i = (jnp.einsum("bhcd,bhde->bhce", q_i * jnp.exp(g_i)[..., None], state)
               + jnp.einsum("bhcm,bhme->bhce", a_i, v_new))
        g_last = g_i[..., -1]
        k_dec = k_i * jnp.exp(g_last[..., None] - g_i)[..., None]
        state = state * jnp.exp(g_last)[..., None, None] + jnp.einsum("bhcd,bhce->bhde", k_dec, v_new)
        return state, o_i

    xs = tuple(jnp.moveaxis(t, 2, 0) for t in (qc, kc, u, w, a_qk, gc))
    s0 = jnp.zeros((b, h, dk, dv), dtype=q.dtype)
    _, o = lax.scan(step, s0, xs)
    return o.transpose(1, 0, 3, 2, 4).reshape(b, l, h, dv)


def parallel_mixers(h, w_in, gdn_conv_qkv, gdn_a_log, gdn_dt_bias, gdn_norm_g, w_gdn_out,
                    conv_dw_w, conv_dw_b, conv_ln_g, conv_ln_b, w_conv_out, b_conv_out, w_mix_out):
    b, l, _ = h.shape
    proj = h @ w_in
    offs = np.cumsum(IN_SIZES)[:-1].tolist()
    qkv, z, a, bt, glu, gates = jnp.split(proj, offs, axis=-1)

    qkv = jax.nn.silu(causal_depthwise_conv(qkv, gdn_conv_qkv))
    q, k, v = jnp.split(qkv, [GDN_KW, 2 * GDN_KW], axis=-1)
    q = q.reshape(b, l, GDN_HEADS, GDN_DK)
    k = k.reshape(b, l, GDN_HEADS, GDN_DK)
    v = v.reshape(b, l, GDN_HEADS, GDN_DV)
    af = a.astype(jnp.float32)
    g = -jnp.exp(gdn_a_log.astype(jnp.float32)) * jax.nn.softplus(af + gdn_dt_bias.astype(jnp.float32))
    beta = jax.nn.sigmoid(bt.astype(jnp.float32))
    o = chunk_gated_delta_rule(q.astype(jnp.float32), k.astype(jnp.float32),
                               v.astype(jnp.float32), g, beta).astype(h.dtype)
    o = rms_norm(o, gdn_norm_g) * jax.nn.silu(z.reshape(b, l, GDN_HEADS, GDN_DV))
    y_a = o.reshape(b, l, GDN_VW) @ w_gdn_out

    c_lin, c_gate = jnp.split(glu, 2, axis=-1)
    c = c_lin * jax.nn.sigmoid(c_gate)
    c = causal_depthwise_conv(c, conv_dw_w) + conv_dw_b
    c = jax.nn.silu(layer_norm(c, conv_ln_g, conv_ln_b))
    y_c = c @ w_conv_out + b_conv_out

    g_a, g_c = jnp.split(gates, 2, axis=-1)
    y = jax.nn.sigmoid(g_a) * y_a + jax.nn.sigmoid(g_c) * y_c
    return y @ w_mix_out


def memory_cross_attention(x, mem, w_xq, w_xkv, w_xo):
    b, l, _ = x.shape
    q = (x @ w_xq).reshape(b, l, XATTN_HEADS, XATTN_HD)
    k, v = jnp.split(mem @ w_xkv, 2, axis=-1)
    k = k.reshape(b, -1, XATTN_HEADS, XATTN_HD)
    v = v.reshape(b, -1, XATTN_HEADS, XATTN_HD)
    s = jnp.einsum("blhd,bmhd->bhlm", q, k).astype(jnp.float32) * (XATTN_HD ** -0.5)
    p = jax.nn.softmax(s, axis=-1).astype(v.dtype)
    o = jnp.einsum("bhlm,bmhd->blhd", p, v).reshape(b, l, D_MODEL)
    return o @ w_xo


def setup_inputs(seed: int = 0) -> dict:
    key = jax.random.key(seed)
    keys = jax.random.split(key, 48)
    counter = [0]

    def nk():
        kk = keys[counter[0]]
        counter[0] += 1
        return kk

    def nrm(shape, scale):
        return jax.random.normal(nk(), shape, jnp.float32) * scale

    def gain(shape):
        return 1.0 + nrm(shape, 0.02)

    L = DEPTH
    dt = jnp.exp(jax.random.uniform(nk(), (L, GDN_HEADS), jnp.float32,
                                    minval=math.log(1e-3), maxval=math.log(0.1)))
    return {
        "x": nrm((BATCH, SEQ, D_MODEL), 1.0),
        "mem": nrm((BATCH, N_MEM, D_MODEL), 1.0),
        "ffn1_wg": nrm((L, D_MODEL, D_FF), D_MODEL ** -0.5),
        "ffn1_wu": nrm((L, D_MODEL, D_FF), D_MODEL ** -0.5),
        "ffn1_wd": nrm((L, D_FF, D_MODEL), DN_BETA * D_FF ** -0.5),
        "ln1_g": gain((L, D_MODEL)),
        "ln1_b": nrm((L, D_MODEL), 0.02),
        "w_in": nrm((L, D_MODEL, IN_WIDTH), D_MODEL ** -0.5),
        "gdn_conv_qkv": nrm((L, SHORT_CONV, QKV_W), SHORT_CONV ** -0.5),
        "gdn_a_log": jnp.log(jax.random.uniform(nk(), (L, GDN_HEADS), jnp.float32, minval=1.0, maxval=16.0)),
        "gdn_dt_bias": dt + jnp.log(-jnp.expm1(-dt)),
        "gdn_norm_g": gain((L, GDN_DV)),
        "w_gdn_out": nrm((L, GDN_VW, D_MODEL), GDN_VW ** -0.5),
        "conv_dw_w": nrm((L, CONV_WIDTH, CONV_CH), CONV_WIDTH ** -0.5),
        "conv_dw_b": nrm((L, CONV_CH), 0.02),
        "conv_ln_g": gain((L, CONV_CH)),
        "conv_ln_b": nrm((L, CONV_CH), 0.02),
        "w_conv_out": nrm((L, CONV_CH, D_MODEL), CONV_CH ** -0.5),
        "b_conv_out": nrm((L, D_MODEL), 0.02),
        "w_mix_out": nrm((L, D_MODEL, D_MODEL), DN_BETA * D_MODEL ** -0.5),
        "ln2_g": gain((L, D_MODEL)),
        "ln2_b": nrm((L, D_MODEL), 0.02),
        "w_xq": nrm((L, D_MODEL, D_MODEL), D_MODEL ** -0.5),
        "w_xkv": nrm((L, D_MODEL, 2 * D_MODEL), D_MODEL ** -0.5),
        "w_xo": nrm((L, D_MODEL, D_MODEL), DN_BETA * D_MODEL ** -0.5),
        "ln3_g": gain((L, D_MODEL)),
        "ln3_b": nrm((L, D_MODEL), 0.02),
        "ffn2_wg": nrm((L, D_MODEL, D_FF), D_MODEL ** -0.5),
        "ffn2_wu": nrm((L, D_MODEL, D_FF), D_MODEL ** -0.5),
        "ffn2_wd": nrm((L, D_FF, D_MODEL), DN_BETA * D_FF ** -0.5),
        "ln4_g": gain((L, D_MODEL)),
        "ln4_b": nrm((L, D_MODEL), 0.02),
    }


def reference(x, mem, ffn1_wg, ffn1_wu, ffn1_wd, ln1_g, ln1_b,
              w_in, gdn_conv_qkv, gdn_a_log, gdn_dt_bias, gdn_norm_g, w_gdn_out,
              conv_dw_w, conv_dw_b, conv_ln_g, conv_ln_b, w_conv_out, b_conv_out, w_mix_out,
              ln2_g, ln2_b, w_xq, w_xkv, w_xo, ln3_g, ln3_b,
              ffn2_wg, ffn2_wu, ffn2_wd, ln4_g, ln4_b):
    for i in range(DEPTH):
        x = layer_norm(DN_ALPHA * x + 0.5 * swiglu_ffn(x, ffn1_wg[i], ffn1_wu[i], ffn1_wd[i]),
                       ln1_g[i], ln1_b[i])
        y = parallel_mixers(x, w_in[i], gdn_conv_qkv[i], gdn_a_log[i], gdn_dt_bias[i], gdn_norm_g[i],
                            w_gdn_out[i], conv_dw_w[i], conv_dw_b[i], conv_ln_g[i], conv_ln_b[i],
                            w_conv_out[i], b_conv_out[i], w_mix_out[i])
        x = layer_norm(DN_ALPHA * x + y, ln2_g[i], ln2_b[i])
        x = layer_norm(DN_ALPHA * x + memory_cross_attention(x, mem, w_xq[i], w_xkv[i], w_xo[i]),
                       ln3_g[i], ln3_b[i])
        x = layer_norm(DN_ALPHA * x + 0.5 * swiglu_ffn(x, ffn2_wg[i], ffn2_wu[i], ffn2_wd[i]),
                       ln4_g[i], ln4_b[i])
    return x
```

```python
import math
import numpy as np
import concourse.bass as bass
import concourse.mybir as mybir
from concourse.bass_utils import run_bass_kernel_spmd
from contextlib import ExitStack

F32 = mybir.dt.float32
BF16 = mybir.dt.bfloat16
F32R = mybir.dt.float32r
AF = mybir.ActivationFunctionType
ALU = mybir.AluOpType

ENGS = ("pe", "act", "dve", "pool", "sp")
D = 1024
KC = 8
DFF = 2816
NMEM = 256
ALPHA = 2.0 ** 0.25
LN_EPS = 1e-5
RMS_EPS = 1e-6
NS = 4
DEBUG_TAGS = None

PV = {}
_o = 0
for _n, _w in [("ln1_g", 8), ("ln1_b", 8), ("ln2_g", 8), ("ln2_b", 8), ("ln3_g", 8), ("ln3_b", 8), ("ln4_g", 8),
               ("ln4_b", 8), ("cln_g", 8), ("cln_b", 8), ("cdw_b", 8), ("bco", 8), ("cdw_w", 248), ("gcq", 96),
               ("gng", 1), ("alog", 8), ("dtb", 8)]:
    PV[_n] = _o
    _o += _w
NPV = _o
CI = {"ident": 0, "triU": 1, "negUs": 2, "posLs": 3, "negUi": 4, "onesm": 5, "ones": 6, "sel127": 7}
NCST = 8


class Dep:
    __slots__ = ("w", "r", "bank")

    def __init__(self, bank=None):
        self.w = None
        self.r = []
        self.bank = bank


class DSem:
    def __init__(self, sem):
        self.sem = sem
        self.cnt = 0


class Prog:
    def __init__(self, nc, es):
        self.nc = nc
        self.es = es
        self.ops = {e: [] for e in ENGS}
        self.sem = {e: es.enter_context(nc.semaphore("s_" + e)) for e in ENGS}
        self.cnt = {e: 0 for e in ENGS}
        self.waited = {e: {} for e in ENGS}
        self.final = []
        self.nsem = 0

    def dsem(self):
        self.nsem += 1
        return DSem(self.es.enter_context(self.nc.semaphore("d%d" % self.nsem)))

    def sb(self, name, shape, dt):
        return self.es.enter_context(self.nc.sbuf_tensor(name, list(shape), dt))

    def ps(self, name, shape, dt):
        return self.es.enter_context(self.nc.psum_tensor(name, list(shape), dt))

    def _need(self, eng, tok, waits):
        if tok is None:
            return
        sem, val, teng = tok
        if teng == eng and eng == "pe":
            return
        k = id(sem)
        if self.waited[eng].get(k, 0) >= val:
            return
        self.waited[eng][k] = val
        waits.append((sem, val))

    def op(self, eng, fn, reads=(), writes=(), dsem=None):
        waits = []
        for d in reads:
            self._need(eng, d.w, waits)
            if d.bank is not None:
                self._need(eng, d.bank[0], waits)
        for d in writes:
            self._need(eng, d.w, waits)
            for t in d.r:
                self._need(eng, t, waits)
        if dsem is None:
            self.cnt[eng] += 1
            tok = (self.sem[eng], self.cnt[eng], eng)
            inc = 1
        else:
            if dsem.cnt > 0:
                self._need(eng, (dsem.sem, dsem.cnt, "dma"), waits)
            dsem.cnt += 16
            tok = (dsem.sem, dsem.cnt, "dma")
            inc = 16
        import sys as _sys
        _f = _sys._getframe(1)
        _tag = []
        for _ in range(3):
            if _f is None:
                break
            _tag.append(_f.f_lineno)
            _f = _f.f_back
        self.ops[eng].append((waits, fn, tok[0], inc, _tag))
        for d in writes:
            d.w = tok
            d.r = []
            if d.bank is not None and eng == "pe":
                d.bank[0] = tok
        for d in reads:
            d.r = [t for t in d.r if t[2] != tok[2] or t[2] == "dma"] + [tok]
        return tok

    def emit(self):
        nc = self.nc
        engmap = {"pe": "tensor", "act": "scalar", "dve": "vector", "pool": "gpsimd", "sp": "sync"}
        fwaits = []
        for d in self.final:
            self._need("sp", d.w, fwaits)
        with nc.Block() as block:
            for e in ENGS:
                ops = self.ops[e]
                extra = fwaits if e == "sp" else []
                if not ops and not extra:
                    continue

                def body(eng, ops=ops, extra=extra):
                    for waits, fn, sem, inc, tag in ops:
                        for (s, v) in waits:
                            eng.wait_ge(s, v)
                        inst = fn(eng)
                        inst.then_inc(sem, inc)
                        if DEBUG_TAGS is not None:
                            DEBUG_TAGS.append((getattr(inst, "name", None), tag))
                    for (s, v) in extra:
                        eng.wait_ge(s, v)

                getattr(block, engmap[e])(body)


def inherit(new, olds):
    toks = []
    for o in olds:
        if o.w is not None:
            toks.append(o.w)
        toks += o.r
    new.r = new.r + toks


WNAMES = ["ffn1_wg", "ffn1_wu", "ffn1_wd", "w_in", "w_gdn_out", "w_conv_out", "w_mix_out", "w_xq", "w_xkv", "w_xo",
          "ffn2_wg", "ffn2_wu", "ffn2_wd"]
WSHAPES = {"ffn1_wg": (D, DFF), "ffn1_wu": (D, DFF), "ffn1_wd": (DFF, D), "w_in": (D, 8208), "w_gdn_out": (D, D),
           "w_conv_out": (D, D), "w_mix_out": (D, D), "w_xq": (D, D), "w_xkv": (D, 2 * D), "w_xo": (D, D),
           "ffn2_wg": (D, DFF), "ffn2_wu": (D, DFF), "ffn2_wd": (DFF, D)}


def build(L, phases=None):
    nc = bass.Bass("TRN2", target_bir_lowering=False)
    T = min(512, L // 2)
    HL = L // 2
    NT = L // T
    NTH = HL // T
    NPH = HL // 128

    dr = {}
    dr["xT"] = nc.dram_tensor("xT", [D, L], F32, kind="ExternalInput").ap()
    dr["memT"] = nc.dram_tensor("memT", [D, NMEM], F32, kind="ExternalInput").ap()
    for n in WNAMES:
        dr[n] = nc.dram_tensor(n, list(WSHAPES[n]), F32, kind="ExternalInput").ap()
    dr["pvec"] = nc.dram_tensor("pvec", [128, NPV], F32, kind="ExternalInput").ap()
    dr["cst"] = nc.dram_tensor("cst", [128, NCST * 128], F32, kind="ExternalInput").ap()
    yT = nc.dram_tensor("yT", [D, L], F32, kind="ExternalOutput").ap()
    Rsp = nc.dram_tensor("Rsp", [128, 8, L], F32, kind="Internal").ap()
    YCsp = nc.dram_tensor("YCsp", [128, 8, L], F32, kind="Internal").ap()

    def fm(ap):
        return ap.rearrange("(c p) n -> p c n", p=128)

    with ExitStack() as es:
        P = Prog(nc, es)
        Rbuf = P.sb("Rbuf", [128, 8 * L], F32)
        R = Rbuf[:].rearrange("p (c l) -> p c l", c=8)
        xb = P.sb("xb", [128, 8, L], BF16)
        Cb = P.sb("Cb", [128, 8, 32 + L], BF16)
        wslots = [P.sb("wsl%d" % i, [128, 4096], BF16) for i in range(NS)]
        wdeps = [Dep() for _ in range(NS)]
        wsems = [P.dsem() for _ in range(NS)]
        pv = P.sb("pv", [128, NPV], F32)
        cs = P.sb("cs", [128, NCST * 128], F32)
        identb = P.sb("identb", [128, 128], BF16)
        pvx = P.sb("pvx", [128, 64], F32)
        NTF = 6
        tfs = [P.sb("tf%d" % i, [128, 512], F32) for i in range(NTF)]
        tfd = [Dep() for _ in range(NTF)]
        tls = [P.sb("tl%d" % i, [128, 512], F32) for i in range(1)]
        tld = [Dep() for _ in range(1)]
        BIG = L >= 2048
        Cflat = Cb[:].rearrange("p c l -> p (c l)")
        if BIG:
            hbufs = [Cflat[:, i * 8 * T:(i + 1) * 8 * T].rearrange("p (c l) -> p c l", c=8) for i in range(2)]
        else:
            hbufs = [P.sb("hb%d" % i, [128, 8, T], BF16)[:] for i in range(2)]
        hdeps = [Dep() for _ in range(2)]
        psb = [P.ps("psb%d" % i, [128, 512], F32) for i in range(8)]
        psd = []
        for _b in range(8):
            _bi = [None]
            psd.append([Dep(_bi) for _ in range(4)])
        cstd = Dep()
        pvd = Dep()
        state = {"wi": 0, "pa": 0, "tf": 0, "pq": 0, "hb": 0, "tl": 0}

        def tl():
            return tls[0], tld[0]

        def C(name):
            i = CI[name]
            return cs[:, i * 128:(i + 1) * 128]

        def pcol(name, j=0, n=1):
            o = PV[name] + j
            return pv[:, o:o + n]

        def tsl(t):
            return slice(t * T, (t + 1) * T)

        def pa():
            i = state["pa"] % 8
            state["pa"] += 1
            return psb[i], psd[i]

        def pq():
            p_, pd_ = pa()
            return p_[:, 0:128], pd_

        def tf():
            i = state["tf"] % NTF
            state["tf"] += 1
            return tfs[i], tfd[i]

        def wload(view3):
            a, b = view3.shape[1], view3.shape[2]
            i = state["wi"] % NS
            state["wi"] += 1
            dst = wslots[i][:, 0:a * b].rearrange("p (a b) -> p a b", a=a)
            P.op("pool", lambda e: e.dma_start(out=dst, in_=view3), writes=[wdeps[i]], dsem=wsems[i])
            return dst, wdeps[i]

        def mm(out, pairs, reads, pdeps, eng="pe"):
            n = len(pairs)

            def fn(e):
                ins = None
                for i, (l, r) in enumerate(pairs):
                    ins = e.matmul(out, lhsT=l, rhs=r, start=(i == 0), stop=(i == n - 1))
                return ins
            P.op("pe", fn, reads=reads, writes=pdeps)

        def act(out, in_, func, reads, writes, **kw):
            P.op("act", lambda e: e.activation(out=out, in_=in_, func=func, **kw), reads=reads, writes=writes)

        def tt(out, in0, in1, op, reads, writes):
            P.op("dve", lambda e: e.tensor_tensor(out=out, in0=in0, in1=in1, op=op), reads=reads, writes=writes)

        def stt(out, in0, scalar, in1, op0, op1, reads, writes):
            P.op("dve", lambda e: e.scalar_tensor_tensor(out=out, in0=in0, scalar=scalar, in1=in1, op0=op0, op1=op1),
                 reads=reads, writes=writes)

        def ts(out, in0, s1, s2, op0, op1, reads, writes):
            if s2 is None:
                P.op("dve", lambda e: e.tensor_scalar(out=out, in0=in0, scalar1=s1, scalar2=None, op0=op0),
                     reads=reads, writes=writes)
            else:
                P.op("dve", lambda e: e.tensor_scalar(out=out, in0=in0, scalar1=s1, scalar2=s2, op0=op0, op1=op1),
                     reads=reads, writes=writes)

        def cp(out, in_, reads, writes):
            P.op("dve", lambda e: e.tensor_copy(out=out, in_=in_), reads=reads, writes=writes)

        s_c = P.dsem()
        s_p = P.dsem()
        P.op("sp", lambda e: e.dma_start(out=cs[:], in_=dr["cst"]), writes=[cstd], dsem=s_c)
        P.op("sp", lambda e: e.dma_start(out=pv[:], in_=dr["pvec"]), writes=[pvd], dsem=s_p)
        identd = Dep()
        cp(identb[:], C("ident"), [cstd], [identd])
        pvxd = Dep()
        for li in range(3):
            o = PV["ln%d_g" % (li + 1)]
            ts(pvx[:, li * 16:li * 16 + 16], pv[:, o:o + 16], ALPHA, None, ALU.mult, None, [pvd], [pvxd])
        act(pvx[:, 48:56], pcol("alog", 0, 8), AF.Exp, [pvd], [pvxd])
        ts(pvx[:, 48:56], pvx[:, 48:56], -1.0, None, ALU.mult, None, [pvxd], [pvxd])
        ident = C("ident")

        Rd = [Dep() for _ in range(NT)]
        xbd = [Dep() for _ in range(NT)]
        lds = [P.dsem() for _ in range(2)]
        xv = fm(dr["xT"])
        for t in range(NT):
            P.op("sp", lambda e, t=t: e.dma_start(out=R[:, :, tsl(t)], in_=xv[:, :, tsl(t)]), writes=[Rd[t]],
                 dsem=lds[t % 2])
            cp(xb[:, :, tsl(t)], R[:, :, tsl(t)], [Rd[t]], [xbd[t]])
            act(R[:, :, tsl(t)], R[:, :, tsl(t)], AF.Copy, [Rd[t]], [Rd[t]], scale=ALPHA)

        def ffn(wg, wu, wd, after_tile=None):
            wgv = dr[wg].rearrange("(kc p) n -> p kc n", p=128)
            wuv = dr[wu].rearrange("(kc p) n -> p kc n", p=128)
            wdv = dr[wd].rearrange("(j p) n -> p j n", p=128)
            for c0 in range(0, DFF, 512):
                gw = min(512, DFF - c0)
                nj = gw // 128
                wgt, wgd = wload(wgv[:, :, c0:c0 + gw])
                wut, wud = wload(wuv[:, :, c0:c0 + gw])
                wdt, wdd = wload(wdv[:, c0 // 128:c0 // 128 + nj, :])
                for t in range(NT):
                    hi = state["hb"] % 2
                    state["hb"] += 1
                    hb, hd = hbufs[hi], hdeps[hi]
                    for j in range(nj):
                        pg, pgd = pa()
                        mm(pg[:, :T], [(wgt[:, kc, j * 128:(j + 1) * 128], xb[:, kc, tsl(t)]) for kc in range(KC)],
                           [wgd, xbd[t]], pgd)
                        pu, pud = pa()
                        mm(pu[:, :T], [(wut[:, kc, j * 128:(j + 1) * 128], xb[:, kc, tsl(t)]) for kc in range(KC)],
                           [wud, xbd[t]], pud)
                        sg, sgd = tf()
                        act(sg[:, :T], pg[:, :T], AF.Silu, pgd, [sgd])
                        stt(hb[:, j, :], sg[:, :T], 0.5, pu[:, :T], ALU.mult, ALU.mult, [sgd] + pud, [hd])
                    for i in range(8):
                        pd_, pdd = pa()
                        mm(pd_[:, :T], [(wdt[:, j, i * 128:(i + 1) * 128], hb[:, j, :]) for j in range(nj)],
                           [wdd, hd], pdd)
                        tt(R[:, i, tsl(t)], R[:, i, tsl(t)], pd_[:, :T], ALU.add, [Rd[t]] + pdd, [Rd[t]])
                    if after_tile is not None and c0 + 512 >= DFF:
                        after_tile(t)

        def mm1(out, l, r, start, stop, reads, pdeps):
            P.op("pe", lambda e: e.matmul(out, lhsT=l, rhs=r, start=start, stop=stop), reads=reads, writes=pdeps)

        def ln_stats(src, srcdeps, n):
            pm, pmd = pa()
            pe2, pe2d = pa()
            mm(pm[:, :n], [(C("onesm"), src(c)) for c in range(8)], srcdeps + [cstd], pmd)
            for c in range(8):
                sq, sqd = tf()
                act(sq[:, :n], src(c), AF.Square, srcdeps, [sqd])
                mm1(pe2[:, :n], C("onesm"), sq[:, :n], c == 0, c == 7, [sqd, cstd], pe2d)
            msq, msqd = tf()
            act(msq[:, :n], pm[:, :n], AF.Square, pmd, [msqd])
            var, vard = tl()
            tt(var[:, :n], pe2[:, :n], msq[:, :n], ALU.subtract, pe2d + [msqd], [vard])
            act(var[:, :n], var[:, :n], AF.Ln, [vard], [vard], bias=LN_EPS)
            act(var[:, :n], var[:, :n], AF.Exp, [vard], [vard], scale=-0.5)
            return pm, pmd, var, vard

        def ln_apply(src, srcdeps, n, pm, pmd, rs, rsd, outs):
            for c in range(8):
                t1, t1d = tf()
                tt(t1[:, :n], src(c), pm[:, :n], ALU.subtract, srcdeps + pmd, [t1d])
                tt(t1[:, :n], t1[:, :n], rs[:, :n], ALU.mult, [t1d, rsd], [t1d])
                for (func, outf, scf, bif, odeps) in outs:
                    act(outf(c), t1[:, :n], func, [t1d, pvd, pvxd], odeps, scale=scf(c), bias=bif(c))

        def layer_norm_resid(li, t):
            src = lambda c: R[:, c, tsl(t)]
            pm, pmd, rs, rsd = ln_stats(src, [Rd[t]], T)
            g0 = PV["ln%d_g" % li]
            b0 = PV["ln%d_b" % li]
            x0 = (li - 1) * 16
            outs = [
                (AF.Identity, lambda c: xb[:, c, tsl(t)], lambda c: pv[:, g0 + c:g0 + c + 1],
                 lambda c: pv[:, b0 + c:b0 + c + 1], [xbd[t]]),
                (AF.Identity, lambda c: R[:, c, tsl(t)], lambda c: pvx[:, x0 + c:x0 + c + 1],
                 lambda c: pvx[:, x0 + 8 + c:x0 + 9 + c], [Rd[t]]),
            ]
            ln_apply(src, [Rd[t]], T, pm, pmd, rs, rsd, outs)

        dbg = {}
        outd = Dep()
        osem = [P.dsem() for _ in range(2)]
        ocnt = [0]

        outds = [Dep() for _ in range(NT)]

        def out_tile(t):
            k = ocnt[0] % 2
            ocnt[0] += 1
            P.op("sp", lambda e, t=t: e.dma_start(out=fm(yT)[:, :, tsl(t)], in_=R[:, :, tsl(t)]),
                 reads=[Rd[t]], writes=[outds[t]], dsem=osem[k])
            P.final.append(outds[t])

        def dump_R_as_out():
            for t in range(NT):
                out_tile(t)

        def mixer():
            winv = dr["w_in"].rearrange("(kc p) n -> p kc n", p=128)
            tslh = lambda a: slice(a * T, (a + 1) * T)
            spd = [Dep() for _ in range(NT)]
            sps = [P.dsem() for _ in range(2)]
            for t in range(NT):
                P.op("sp", lambda e, t=t: e.dma_start(out=Rsp[:, :, tsl(t)], in_=R[:, :, tsl(t)]), reads=[Rd[t]],
                     writes=[spd[t]], dsem=sps[t % 2])
            Cd = [Dep() for _ in range(NT)]
            Chd = Dep()
            for d in Cd + [Chd]:
                inherit(d, hdeps)
            P.op("dve", lambda e: e.memset(Cb[:, :, 0:32], 0.0), writes=[Chd])
            wls = [wload(winv[:, :, 4112 + g * 512:4112 + (g + 1) * 512]) for g in range(2)]
            wgs = [wload(winv[:, :, 5136 + g * 512:5136 + (g + 1) * 512]) for g in range(2)]
            for t in range(NT):
                for g in range(2):
                    wl, wld = wls[g]
                    wg_, wgd_ = wgs[g]
                    for j in range(4):
                        pl, pld = pa()
                        mm(pl[:, :T], [(wl[:, kc, j * 128:(j + 1) * 128], xb[:, kc, tsl(t)]) for kc in range(KC)],
                           [wld, xbd[t]], pld)
                        pg, pgd = pa()
                        mm(pg[:, :T], [(wg_[:, kc, j * 128:(j + 1) * 128], xb[:, kc, tsl(t)]) for kc in range(KC)],
                           [wgd_, xbd[t]], pgd)
                        sg, sgd = tf()
                        act(sg[:, :T], pg[:, :T], AF.Sigmoid, pgd, [sgd])
                        tt(Cb[:, g * 4 + j, 32 + t * T:32 + (t + 1) * T], sg[:, :T], pl[:, :T], ALU.mult, [sgd] + pld,
                           [Cd[t]])
            C2 = Rbuf[:, 0:8 * HL].rearrange("p (c l) -> p c l", c=8)
            csb = Rbuf[:, 8 * HL:8 * HL + 4 * T].bitcast(BF16).rearrange("p (c l) -> p c l", c=8)
            C2d = [Dep() for _ in range(NTH)]
            csd = Dep()
            for d in C2d + [csd]:
                inherit(d, Rd)
            Dg = P.sb("Dg", [128, 32, 128], BF16)
            Dgd = Dep()
            DgdA, DgdB = Dep(), Dep()
            wcov = dr["w_conv_out"].rearrange("(kc p) n -> p kc n", p=128)
            ycd = [Dep() for _ in range(NT)]
            ycs = [P.dsem() for _ in range(2)]
            for hh in range(2):
                for c in range(8):
                    halves = [(0, 16, DgdA), (16, 31, DgdB)]
                    pss = {}
                    for (k0, k1, dgd_) in halves:
                        for k in range(k0, k1):
                            ts(Dg[:, k, :], identb[:], pcol("cdw_w", c * 31 + k), None, ALU.mult, None, [identd, pvd], [dgd_])
                        for a in range(NTH):
                            t = hh * NTH + a
                            if k0 == 0:
                                pss[a] = pa()
                            p_, pd_ = pss[a]
                            base = 32 + t * T - 30
                            rds = [dgd_, Cd[t], Chd] + ([Cd[t - 1]] if t > 0 else [])
                            n_ = k1 - k0

                            def fn(e, p_=p_, c=c, base=base, k0=k0, k1=k1):
                                ins = None
                                for k in range(k0, k1):
                                    ins = e.matmul(p_[:, :T], lhsT=Dg[:, k, :], rhs=Cb[:, c, base + k:base + k + T],
                                                   start=(k == 0), stop=(k == 30))
                                return ins
                            P.op("pe", fn, reads=rds, writes=pd_)
                    for a in range(NTH):
                        p_, pd_ = pss[a]
                        act(C2[:, c, tslh(a)], p_[:, :T], AF.Identity, pd_ + [pvd], [C2d[a]], bias=pcol("cdw_b", c))
                wco = [wload(wcov[:, :, g * 512:(g + 1) * 512]) for g in range(2)]
                wgc = [wload(winv[:, :, 7184 + g * 512:7184 + (g + 1) * 512]) for g in range(2)]
                for a in range(NTH):
                    t = hh * NTH + a
                    src = lambda c, a=a: C2[:, c, tslh(a)]
                    pm, pmd, rs, rsd = ln_stats(src, [C2d[a]], T)
                    outs = [(AF.Silu, lambda c: csb[:, c, :], lambda c: pcol("cln_g", c), lambda c: pcol("cln_b", c),
                             [csd])]
                    ln_apply(src, [C2d[a]], T, pm, pmd, rs, rsd, outs)
                    for i in range(8):
                        py, pyd = pa()
                        mm(py[:, :T], [(wco[i // 4][0][:, kc, (i % 4) * 128:(i % 4 + 1) * 128], csb[:, kc, :])
                                       for kc in range(KC)], [wco[i // 4][1], csd], pyd)
                        pg, pgd = pa()
                        mm(pg[:, :T], [(wgc[i // 4][0][:, kc, (i % 4) * 128:(i % 4 + 1) * 128], xb[:, kc, tsl(t)])
                                       for kc in range(KC)], [wgc[i // 4][1], xbd[t]], pgd)
                        sg, sgd = tf()
                        act(sg[:, :T], pg[:, :T], AF.Sigmoid, pgd, [sgd])
                        stt(C2[:, i, tslh(a)], py[:, :T], pcol("bco", i), sg[:, :T], ALU.add, ALU.mult,
                            pyd + [sgd, pvd], [C2d[a]])
                    P.op("sp", lambda e, t=t, a=a: e.dma_start(out=YCsp[:, :, tsl(t)], in_=C2[:, :, tslh(a)]),
                         reads=[C2d[a]], writes=[ycd[t]], dsem=ycs[t % 2])

            Rb16 = Rbuf[:].bitcast(BF16)
            QT = Rb16[:, 0:8 * HL].rearrange("p (c l) -> p c l", c=8)
            KT = Rb16[:, 8 * HL:16 * HL].rearrange("p (c l) -> p c l", c=8)
            VT = Rb16[:, 16 * HL:24 * HL].rearrange("p (c l) -> p c l", c=8)
            ZS = Rb16[:, 24 * HL:32 * HL].rearrange("p (c l) -> p c l", c=8)
            qd = [Dep() for _ in range(NTH)]
            kd = [Dep() for _ in range(NTH)]
            vd = [Dep() for _ in range(NTH)]
            zd = [Dep() for _ in range(NTH)]
            for d in qd + kd + vd + zd:
                inherit(d, C2d + [csd] + Rd)
            halo = P.sb("halo", [128, 24, 4], F32)
            halod = Dep()
            P.op("dve", lambda e: e.memset(halo[:], 0.0), writes=[halod])
            Dgflat = Dg[:].rearrange("p a b -> p (a b)")
            stg = [Dgflat[:, i * 1040:i * 1040 + 1032].bitcast(F32) for i in range(2)]
            stgd = [Dep(), Dep()]
            for d in stgd:
                inherit(d, [Dgd, DgdA, DgdB])
            gtm = P.sb("gtm", [128, NPH, 8], F32)
            lbt = P.sb("lbt", [128, NPH, 8], F32)
            bet = P.sb("bet", [128, NPH, 8], F32)
            gcc = P.sb("gcc", [128, NPH, 8], F32)
            glc = P.sb("glc", [128, NPH, 8], F32)
            egl8 = P.sb("egl8", [128, NPH, 8], F32)
            dcol8 = P.sb("dcol8", [128, NPH, 8], F32)
            tmp8 = P.sb("tmp8", [128, 8], F32)
            tmp8d = Dep()
            x8 = P.sb("x8", [128, 16], F32)
            scd, x8d = Dep(), Dep()
            S = P.sb("S", [128, 8, 128], F32)
            Sb = P.sb("Sb", [128, 8, 128], BF16)
            Sd = [Dep() for _ in range(8)]
            P.op("dve", lambda e: e.memset(S[:], 0.0), writes=Sd)
            P.op("dve", lambda e: e.memset(Sb[:], 0.0), reads=[], writes=Sd)
            NSLOT = 4
            NGF, NGB, NSM, NGR = 5, 5, 8, 4
            grb = P.sb("grb", [128, NSLOT * NGR * 128], F32R)
            slot_gr = [[grb[:, (sl_ * NGR + i) * 128:(sl_ * NGR + i + 1) * 128] for i in range(NGR)] for sl_ in range(NSLOT)]
            slot_grd = [[Dep() for _ in range(NGR)] for _ in range(NSLOT)]
            Dgf = Dg[:].rearrange("p a b -> p (a b)")
            slot_gf, slot_gfd, slot_gb, slot_gbd = [], [], [], []
            slot_gf.append([Dgf[:, i * 256:(i + 1) * 256].bitcast(F32) for i in range(NGF)])
            slot_gb.append([Dgf[:, NGF * 256 + i * 128:NGF * 256 + (i + 1) * 128] for i in range(NGB)])
            tfq = []
            for i in (0, 1, 2, 5):
                for q_ in range(4):
                    tfq.append(tfs[i][:, q_ * 128:(q_ + 1) * 128])
            gbhost = [tfs[3], tfs[4], tls[0]]
            for sl_ in range(1, NSLOT):
                slot_gf.append(tfq[(sl_ - 1) * NGF:sl_ * NGF])
                tfb = gbhost[sl_ - 1][:].bitcast(BF16)
                slot_gb.append([tfb[:, i * 128:(i + 1) * 128] for i in range(NGB)])
            for sl_ in range(NSLOT):
                slot_gfd.append([Dep() for _ in range(NGF)])
                slot_gbd.append([Dep() for _ in range(NGB)])
            for d in slot_gfd[0] + slot_gbd[0]:
                inherit(d, [Dgd, DgdA, DgdB])
            gfd = slot_gfd[0]
            gbd = slot_gbd[0]
            sms = P.sb("sms", [128, NSLOT * NSM], F32)
            smd = [Dep() for _ in range(NSLOT * NSM)]
            st2 = {"stg": 0}
            for sl_ in range(NSLOT):
                st2["gf%d" % sl_] = 0
                st2["gb%d" % sl_] = 0
                st2["sm%d" % sl_] = 0
                st2["gr%d" % sl_] = 0

            def grx(sl_):
                i = st2["gr%d" % sl_] % NGR
                st2["gr%d" % sl_] += 1
                return slot_gr[sl_][i], slot_grd[sl_][i]

            def gfx(sl_):
                i = st2["gf%d" % sl_] % NGF
                st2["gf%d" % sl_] += 1
                return slot_gf[sl_][i], slot_gfd[sl_][i]

            def gbx(sl_):
                i = st2["gb%d" % sl_] % NGB
                st2["gb%d" % sl_] += 1
                return slot_gb[sl_][i], slot_gbd[sl_][i]

            def smx(sl_):
                i = st2["sm%d" % sl_] % NSM
                st2["sm%d" % sl_] += 1
                return sms[:, sl_ * NSM + i:sl_ * NSM + i + 1], smd[sl_ * NSM + i]

            seg_dst = [(QT, qd), (KT, kd), (VT, vd), (ZS, zd)]
            for hh in range(2):
                for seg in range(4):
                    dst, dstd = seg_dst[seg]
                    for g in range(2):
                        wt, wtd = wload(winv[:, :, seg * 1024 + g * 512:seg * 1024 + (g + 1) * 512])
                        for j in range(4):
                            ch = g * 4 + j
                            cc = seg * 8 + ch
                            tiles = list(range(NTH))
                            ps_ = {}
                            for a in tiles:
                                t = hh * NTH + a
                                p_, pd_ = pa()
                                mm(p_[:, :T], [(wt[:, kc, j * 128:(j + 1) * 128], xb[:, kc, tsl(t)]) for kc in range(KC)],
                                   [wtd, xbd[t]], pd_)
                                ps_[a] = (p_, pd_)
                            if seg == 3:
                                for a in tiles:
                                    p_, pd_ = ps_[a]
                                    act(dst[:, ch, tslh(a)], p_[:, :T], AF.Silu, pd_, [dstd[a]])
                                continue
                            sts = {}
                            for a in tiles:
                                p_, pd_ = ps_[a]
                                si = st2["stg"] % 2
                                st2["stg"] += 1
                                st, std = stg[si], stgd[si]
                                act(st[:, 3:3 + T], p_[:, :T], AF.Copy, pd_, [std])
                                sts[a] = (st, std)
                            for a in tiles:
                                st, std = sts[a]
                                cp(st[:, 0:3], halo[:, cc, 0:3], [halod], [std])
                                cp(halo[:, cc, 0:3], st[:, T:T + 3], [std], [halod])
                            accs = {}
                            for a in tiles:
                                st, std = sts[a]
                                acc, accd = tf()
                                act(acc[:, :T], st[:, 3:3 + T], AF.Identity, [std, pvd], [accd],
                                    scale=pcol("gcq", cc * 4 + 3), bias=0.0)
                                accs[a] = (acc, accd)
                            for k in range(3):
                                for a in tiles:
                                    st, std = sts[a]
                                    acc, accd = accs[a]
                                    stt(acc[:, :T], st[:, k:k + T], pcol("gcq", cc * 4 + k), acc[:, :T], ALU.mult, ALU.add,
                                        [std, accd, pvd], [accd])
                            if seg == 2:
                                for a in tiles:
                                    acc, accd = accs[a]
                                    act(dst[:, ch, tslh(a)], acc[:, :T], AF.Silu, [accd], [dstd[a]])
                                continue
                            qss, sqq, pns = {}, {}, {}
                            for a in tiles:
                                acc, accd = accs[a]
                                act(acc[:, :T], acc[:, :T], AF.Silu, [accd], [accd])
                            for a in tiles:
                                acc, accd = accs[a]
                                sq, sqd = tf()
                                act(sq[:, :T], acc[:, :T], AF.Square, [accd], [sqd])
                                sqq[a] = (sq, sqd)
                            for a in tiles:
                                sq, sqd = sqq[a]
                                pn, pnd = pa()
                                mm(pn[:, :T], [(C("ones"), sq[:, :T])], [sqd, cstd], pnd)
                                pns[a] = (pn, pnd)
                            for a in tiles:
                                sq, sqd = sqq[a]
                                pn, pnd = pns[a]
                                act(sq[:, :T], pn[:, :T], AF.Ln, pnd, [sqd], bias=RMS_EPS)
                            for a in tiles:
                                sq, sqd = sqq[a]
                                act(sq[:, :T], sq[:, :T], AF.Exp, [sqd], [sqd], scale=-0.5,
                                    bias=(-0.5 * math.log(128.0) if seg == 0 else 0.0))
                            for a in tiles:
                                acc, accd = accs[a]
                                sq, sqd = sqq[a]
                                tt(dst[:, ch, tslh(a)], acc[:, :T], sq[:, :T], ALU.mult, [accd, sqd], [dstd[a]])
                wab, wabd = wload(winv[:, :, 4096:4112])
                for pp in range(NPH):
                    gp = hh * NPH + pp
                    tok = slice(gp * 128, (gp + 1) * 128)
                    t = (gp * 128) // T
                    pab, pabd = pq()
                    mm(pab[:, 0:16], [(xb[:, kc, tok], wab[:, kc, :]) for kc in range(KC)], [xbd[t], wabd], pabd)
                    tt(x8[:, 0:8], pab[:, 0:8], pcol("dtb", 0, 8), ALU.add, pabd + [pvd], [x8d])
                    act(x8[:, 0:8], x8[:, 0:8], AF.Exp, [x8d], [x8d])
                    act(x8[:, 0:8], x8[:, 0:8], AF.Ln, [x8d], [x8d], bias=1.0)
                    tt(gtm[:, pp, :], x8[:, 0:8], pvx[:, 48:56], ALU.mult, [x8d, pvxd], [scd])
                    act(x8[:, 8:16], pab[:, 8:16], AF.Exp, pabd, [x8d], scale=-1.0)
                    act(x8[:, 8:16], x8[:, 8:16], AF.Ln, [x8d], [x8d], bias=1.0)
                    ts(lbt[:, pp, :], x8[:, 8:16], -1.0, None, ALU.mult, None, [x8d], [scd])
                    act(bet[:, pp, :], lbt[:, pp, :], AF.Exp, [scd], [scd])
                    pg_, pgd_ = pq()
                    mm(pg_[:, 0:8], [(C("triU"), gtm[:, pp, :])], [scd, cstd], pgd_)
                    cp(gcc[:, pp, :], pg_[:, 0:8], pgd_, [scd])
                    tt(glc[:, pp, :], gcc[:, pp, :], lbt[:, pp, :], ALU.add, [scd], [scd])
                    pgl, pgld = pq()
                    mm(pgl[:, 0:8], [(C("sel127"), gcc[:, pp, :])], [scd, cstd], pgld)
                    act(egl8[:, pp, :], pgl[:, 0:8], AF.Exp, pgld, [scd])
                    tt(tmp8[:], pgl[:, 0:8], gcc[:, pp, :], ALU.subtract, pgld + [scd], [tmp8d])
                    act(dcol8[:, pp, :], tmp8[:], AF.Exp, [tmp8d], [scd])
                for sl_ in range(1, NSLOT):
                    for d in slot_gfd[sl_] + slot_gbd[sl_]:
                        inherit(d, tfd + tld)
                for d in slot_gfd[0] + slot_gbd[0]:
                    inherit(d, stgd)

                def head_gen(pp, h, sl_):
                    gp = hh * NPH + pp
                    a = (pp * 128) // T
                    t = (gp * 128) // T
                    hs = slice(pp * 128, (pp + 1) * 128)
                    otok = slice(32 + gp * 128, 32 + (gp + 1) * 128)
                    gf = lambda: gfx(sl_)
                    gr = lambda: grx(sl_)
                    f32v = lambda ap: ap.bitcast(F32)
                    gb = lambda: gbx(sl_)
                    sm = lambda: smx(sl_)
                    kT, qT, vT = KT[:, h, hs], QT[:, h, hs], VT[:, h, hs]
                    Gb_, Gbd_ = gf()
                    cp(Gb_, gtm[:, pp, h:h + 1].to_broadcast([128, 128]), [scd], [Gbd_])
                    Lb_, Lbd_ = gf()
                    cp(Lb_, lbt[:, pp, h:h + 1].to_broadcast([128, 128]), [scd], [Lbd_])
                    yield
                    P2, P2d = pq()
                    mm(P2, [(Gb_, C("triU"))], [Gbd_, cstd], P2d)
                    P1, P1d = pq()
                    mm(P1, [(Gb_, C("triU")), (Lb_, ident)], [Gbd_, Lbd_, cstd], P1d)
                    gcol = gcc[:, pp, h:h + 1]
                    glcol = glc[:, pp, h:h + 1]
                    t1, t1d = gf()
                    stt(t1, P1, gcol, C("negUs"), ALU.subtract, ALU.add, P1d + [scd, cstd], [t1d])
                    act(t1, t1, AF.Exp, [t1d], [t1d])
                    t2, t2d = gf()
                    stt(t2, P2, glcol, C("posLs"), ALU.subtract, ALU.add, P2d + [scd, cstd], [t2d])
                    act(t2, t2, AF.Exp, [t2d], [t2d], scale=-1.0)
                    t3, t3d = gf()
                    stt(t3, P2, gcol, C("negUi"), ALU.subtract, ALU.add, P2d + [scd, cstd], [t3d])
                    act(t3, t3, AF.Exp, [t3d], [t3d])
                    E1, E1d = gf()
                    act(E1, P1, AF.Exp, P1d, [E1d])
                    kbg, kbgd = gb()
                    tt(kbg, kT, E1, ALU.mult, [kd[a], E1d], [kbgd])
                    E2, E2d = gf()
                    act(E2, P2, AF.Exp, P2d, [E2d])
                    qg, qgd = gb()
                    tt(qg, qT, E2, ALU.mult, [qd[a], E2d], [qgd])
                    egl, egld = egl8[:, pp, h:h + 1], scd
                    dcol, dcold = dcol8[:, pp, h:h + 1], scd
                    yield
                    Gk, Gkd = pq()
                    mm(Gk, [(kT, kT)], [kd[a]], Gkd)
                    Gq, Gqd = pq()
                    mm(Gq, [(kT, qT)], [kd[a], qd[a]], Gqd)
                    Nn, Nd = gr()
                    tt(Nn, Gk, t1, ALU.mult, Gkd + [t1d], [Nd])
                    Mm, Md = gr()
                    tt(Mm, Gk, t2, ALU.mult, Gkd + [t2d], [Md])
                    AT, ATd = gb()
                    tt(AT, Gq, t3, ALU.mult, Gqd + [t3d], [ATd])
                    Y, Yd = gr()
                    tt(Y, ident, f32v(Nn), ALU.subtract, [cstd, Nd], [Yd])
                    yield
                    Mp, Mpd, Np, Npd = Mm, Md, Nn, Nd
                    for k in range(1, 7):
                        pM, pMd = pq()
                        mm(pM, [(Np, Mp)], [Npd, Mpd], pMd)
                        Mp2, Mp2d = gr()
                        act(Mp2, pM, AF.Copy, pMd, [Mp2d])
                        if k < 6:
                            pN, pNd = pq()
                            mm(pN, [(Mp, Np)], [Npd, Mpd], pNd)
                            Np2, Np2d = gr()
                            cp(Np2, pN, pNd, [Np2d])
                        yield
                        pY, pYd = pq()
                        mm(pY, [(Mp2, Y)], [Mp2d, Yd], pYd)
                        Y2, Y2d = gr()
                        tt(Y2, f32v(Y), pY, ALU.add, [Yd] + pYd, [Y2d])
                        Mp, Mpd, Y, Yd = Mp2, Mp2d, Y2, Y2d
                        if k < 6:
                            Np, Npd = Np2, Np2d
                        yield
                    pk, pkd = pq()
                    pkb = pk.bitcast(BF16)[:, 0:128]
                    P.op("pe", lambda e, pkb=pkb, kT=kT: e.transpose(out=pkb, in_=kT, identity=identb[:]),
                         reads=[kd[a], identd], writes=pkd)
                    kdec, kdecd = gb()
                    act(kdec, pkb, AF.Copy, pkd + [dcold], [kdecd], scale=dcol)
                    pvv, pvvd = pq()
                    pvb = pvv.bitcast(BF16)[:, 0:128]
                    P.op("pe", lambda e, pvb=pvb, vT=vT: e.transpose(out=pvb, in_=vT, identity=identb[:]),
                         reads=[vd[a], identd], writes=pvvd)
                    vbt, vbtd = gf()
                    ts(vbt, pvb, bet[:, pp, h:h + 1], None, ALU.mult, None, pvvd + [scd], [vbtd])
                    yield
                    pR, pRd = pq()
                    mm(pR, [(kbg, Sb[:, h, :])], [kbgd, Sd[h]], pRd)
                    Rs_, Rsd_ = gr()
                    tt(Rs_, vbt, pR, ALU.subtract, [vbtd] + pRd, [Rsd_])
                    yield
                    pvn, pvnd = pq()
                    mm(pvn, [(Y, Rs_)], [Yd, Rsd_], pvnd)
                    vnb, vnbd = gb()
                    act(vnb, pvn, AF.Copy, pvnd, [vnbd])
                    yield
                    po, pod = pq()
                    mm(po, [(qg, Sb[:, h, :]), (AT, vnb)], [qgd, Sd[h], ATd, vnbd], pod)
                    pS, pSd = pq()
                    mm(pS, [(kdec, vnb)], [kdecd, vnbd], pSd)
                    stt(S[:, h, :], S[:, h, :], egl, pS, ALU.mult, ALU.add, [egld] + pSd, [Sd[h]])
                    act(Sb[:, h, :], S[:, h, :], AF.Copy, [Sd[h]], [Sd[h]])
                    junk, junkd = gf()
                    ss, ssd = sm()
                    P.op("act", lambda e, junk=junk, po=po, ss=ss: e.activation(out=junk, in_=po, func=AF.Square,
                                                                                accum_out=ss),
                         reads=pod, writes=[junkd, ssd])
                    ts(ss, ss, 1.0 / 128.0, RMS_EPS, ALU.mult, ALU.add, [ssd], [ssd])
                    act(ss, ss, AF.Ln, [ssd], [ssd])
                    act(ss, ss, AF.Exp, [ssd], [ssd], scale=-0.5)
                    on, ond = gf()
                    act(on, po, AF.Copy, pod + [ssd], [ond], scale=ss)
                    yield
                    pT, pTd = pq()
                    P.op("pe", lambda e, pT=pT, on=on: e.transpose(out=pT, in_=on, identity=ident),
                         reads=[ond, cstd], writes=pTd)
                    stt(Cb[:, h, otok], pT, pcol("gng"), ZS[:, h, hs], ALU.mult, ALU.mult, pTd + [pvd, zd[a]],
                        [Cd[t]])
                    yield

                work = [(pp, h) for pp in range(NPH) for h in range(8)]
                active = {}
                while work or active:
                    for sl_ in range(NSLOT):
                        if sl_ not in active and work:
                            pp_, h_ = work.pop(0)
                            active[sl_] = head_gen(pp_, h_, sl_)
                    for sl_ in list(active):
                        try:
                            next(active[sl_])
                        except StopIteration:
                            del active[sl_]
                for sl_ in range(1, NSLOT):
                    for d in tfd + tld:
                        inherit(d, slot_gfd[sl_] + slot_gbd[sl_])
                for d in stgd:
                    inherit(d, slot_gfd[0] + slot_gbd[0])

            Ytmp = Dg[:].rearrange("p a b -> p (a b)").rearrange("p (c l) -> p c l", c=8)[:, :, 0:T]
            Ytd = Dep()
            inherit(Ytd, [Dgd, DgdA, DgdB] + gfd + gbd)
            wgov = dr["w_gdn_out"].rearrange("(kc p) n -> p kc n", p=128)
            wmov = dr["w_mix_out"].rearrange("(kc p) n -> p kc n", p=128)
            wgo = [wload(wgov[:, :, g * 512:(g + 1) * 512]) for g in range(2)]
            wga = [wload(winv[:, :, 6160 + g * 512:6160 + (g + 1) * 512]) for g in range(2)]
            inherit(Ytd, stgd)
            ycbs = [P.dsem() for _ in range(2)]
            for t in range(NT):
                for i in range(8):
                    pya, pyad = pa()
                    mm(pya[:, :T], [(wgo[i // 4][0][:, kc, (i % 4) * 128:(i % 4 + 1) * 128], Cb[:, kc, 32 + t * T:32 + (t + 1) * T])
                                    for kc in range(KC)], [wgo[i // 4][1], Cd[t]], pyad)
                    pga, pgad = pa()
                    mm(pga[:, :T], [(wga[i // 4][0][:, kc, (i % 4) * 128:(i % 4 + 1) * 128], xb[:, kc, tsl(t)])
                                    for kc in range(KC)], [wga[i // 4][1], xbd[t]], pgad)
                    sg, sgd = tf()
                    act(sg[:, :T], pga[:, :T], AF.Sigmoid, pgad, [sgd])
                    bi = (t * 8 + i) % 2
                    ycb_, ycbd_ = tf()
                    ycb = ycb_[:, 0:T]
                    P.op("sp", lambda e, ycb=ycb, i=i, t=t: e.dma_start(out=ycb, in_=YCsp[:, i, tsl(t)]),
                         reads=[ycd[t]], writes=[ycbd_], dsem=ycbs[bi])
                    tt(sg[:, :T], pya[:, :T], sg[:, :T], ALU.mult, pyad + [sgd], [sgd])
                    tt(Ytmp[:, i, :], sg[:, :T], ycb, ALU.add, [sgd, ycbd_], [Ytd])
                act(Cb[:, :, 32 + t * T:32 + (t + 1) * T], Ytmp, AF.Copy, [Ytd], [Cd[t]])
            wmo = [wload(wmov[:, :, g * 512:(g + 1) * 512]) for g in range(2)]
            rls = [P.dsem() for _ in range(2)]
            for t in range(NT):
                inherit(Rd[t], qd + kd + vd + zd + C2d + [csd])
                P.op("sp", lambda e, t=t: e.dma_start(out=R[:, :, tsl(t)], in_=Rsp[:, :, tsl(t)]), reads=[spd[t]],
                     writes=[Rd[t]], dsem=rls[t % 2])
                for i in range(8):
                    p_, pd_ = pa()
                    mm(p_[:, :T], [(wmo[i // 4][0][:, kc, (i % 4) * 128:(i % 4 + 1) * 128], Cb[:, kc, 32 + t * T:32 + (t + 1) * T])
                                   for kc in range(KC)], [wmo[i // 4][1], Cd[t]], pd_)
                    tt(R[:, i, tsl(t)], R[:, i, tsl(t)], p_[:, :T], ALU.add, [Rd[t]] + pd_, [Rd[t]])
                if t > 0:
                    layer_norm_resid(2, t - 1)
            layer_norm_resid(2, NT - 1)
            for d in hdeps + etd + [memd, KTd, Vd]:
                inherit(d, Cd + [Chd])

        if BIG:
            o0 = 20 * T
            memb = Cflat[:, o0:o0 + 2048].rearrange("p (c l) -> p c l", c=8)
            KTb = Cflat[:, o0 + 2048:o0 + 4096].rearrange("p (c l) -> p c l", c=8)
            Vb = Cflat[:, o0 + 4096:o0 + 6144].rearrange("p (c l) -> p c l", c=2)
        else:
            memb = P.sb("memb", [128, 8, NMEM], BF16)[:]
            KTb = P.sb("KTb", [128, 8, NMEM], BF16)[:]
            Vb = P.sb("Vb", [128, 2, D], BF16)[:]
        onesb = P.sb("onesb", [128, 128], BF16)
        if BIG:
            etb = [Cflat[:, 16 * T + i * 2 * T:16 * T + (i + 1) * 2 * T].rearrange("p (c l) -> p c l", c=2) for i in range(2)]
        else:
            etb = [P.sb("etb%d" % i, [128, 2, T], BF16)[:] for i in range(2)]
        etd = [Dep() for _ in range(2)]
        kms = P.sb("kms", [1, 8], F32)
        memd, KTd, Vd, onesbd, kmsd = Dep(), Dep(), Dep(), Dep(), Dep()

        def xattn_prep():
            s_m = P.dsem()
            P.op("pool", lambda e: e.dma_start(out=memb, in_=fm(dr["memT"])), writes=[memd], dsem=s_m)
            cp(onesb[:], C("ones"), [cstd], [onesbd])
            wv = dr["w_xkv"].rearrange("(kc p) n -> p kc n", p=128)
            for g in range(2):
                wt, wd_ = wload(wv[:, :, g * 512:(g + 1) * 512])
                for j in range(4):
                    p_, pd_ = pa()
                    mm(p_[:, :NMEM], [(wt[:, kc, j * 128:(j + 1) * 128], memb[:, kc, :]) for kc in range(KC)],
                       [wd_, memd], pd_)
                    act(KTb[:, g * 4 + j, :], p_[:, :NMEM], AF.Copy, pd_, [KTd])
            for g in range(2):
                wt, wd_ = wload(wv[:, :, D + g * 512:D + (g + 1) * 512])
                for mc in range(2):
                    p_, pd_ = pa()
                    mm(p_[:, :512], [(memb[:, kc, mc * 128:(mc + 1) * 128], wt[:, kc, :]) for kc in range(KC)],
                       [wd_, memd], pd_)
                    act(Vb[:, mc, g * 512:(g + 1) * 512], p_[:, :512], AF.Copy, pd_, [Vd])
            for h in range(4):
                sqs = []
                for dc in range(2):
                    sq, sqd = tf()
                    act(sq[:, :NMEM], KTb[:, 2 * h + dc, :], AF.Square, [KTd], [sqd])
                    sqs.append((sq, sqd))
                p_, pd_ = pa()
                mm(p_[0:1, :NMEM], [(C("ones")[:, 0:1], sq[:, :NMEM]) for sq, _ in sqs], [d for _, d in sqs] + [cstd], pd_)
                P.op("dve", lambda e, p_=p_, h=h: e.tensor_reduce(out=kms[0:1, h:h + 1], in_=p_[0:1, :NMEM],
                                                                 axis=mybir.AxisListType.X, op=ALU.max),
                     reads=pd_, writes=[kmsd])
            act(kms[0:1, 0:4], kms[0:1, 0:4], AF.Sqrt, [kmsd], [kmsd])
            ts(kms[0:1, 0:4], kms[0:1, 0:4], -1.0, None, ALU.mult, None, [kmsd], [kmsd])

        def xattn_tile(t, wq, wqd, wo, wod):
            hq, hqd = hbufs[0], hdeps[0]
            ho, hod = hbufs[1], hdeps[1]
            for i in range(8):
                p_, pd_ = pa()
                mm(p_[:, :T], [(wq[i // 4][:, kc, (i % 4) * 128:(i % 4 + 1) * 128], xb[:, kc, tsl(t)]) for kc in range(KC)],
                   [wqd[i // 4], xbd[t]], pd_)
                act(hq[:, i, :], p_[:, :T], AF.Copy, pd_, [hqd])
            for h in range(4):
                sqs = []
                for dc in range(2):
                    sq, sqd = tf()
                    act(sq[:, :T], hq[:, 2 * h + dc, :], AF.Square, [hqd], [sqd])
                    sqs.append((sq, sqd))
                p_, pd_ = pa()
                mm(p_[0:1, :T], [(C("ones")[:, 0:1], sq[:, :T]) for sq, _ in sqs], [d for _, d in sqs] + [cstd], pd_)
                rowt_, rowd = tf()
                rowt = rowt_[0:1, :T]
                act(rowt, p_[0:1, :T], AF.Sqrt, pd_, [rowd])
                ts(rowt, rowt, kms[0:1, h:h + 1], None, ALU.mult, None, [rowd, kmsd], [rowd])
                ei = (t * 4 + h) % 2
                et, etdd = etb[ei], etd[ei]
                for mc in range(2):
                    p_, pd_ = pa()
                    msl = slice(mc * 128, (mc + 1) * 128)
                    mm(p_[:, :T], [(KTb[:, 2 * h, msl], hq[:, 2 * h, :]), (KTb[:, 2 * h + 1, msl], hq[:, 2 * h + 1, :]),
                                   (C("ones")[0:1, :], rowt)], [KTd, hqd, rowd, cstd], pd_)
                    act(et[:, mc, :], p_[:, :T], AF.Exp, pd_, [etdd], scale=1.0 / 16.0)
                pden, pdend = pa()
                mm(pden[:, :T], [(onesb[:], et[:, mc, :]) for mc in range(2)], [onesbd, etdd], pdend)
                rd, rdd = tf()
                P.op("dve", lambda e, rd=rd, pden=pden: e.reciprocal(out=rd[:, :T], in_=pden[:, :T]), reads=pdend,
                     writes=[rdd])
                for dc in range(2):
                    p_, pd_ = pa()
                    c0 = (2 * h + dc) * 128
                    mm(p_[:, :T], [(Vb[:, mc, c0:c0 + 128], et[:, mc, :]) for mc in range(2)], [Vd, etdd], pd_)
                    tt(ho[:, 2 * h + dc, :], p_[:, :T], rd[:, :T], ALU.mult, pd_ + [rdd], [hod])
            for i in range(8):
                p_, pd_ = pa()
                mm(p_[:, :T], [(wo[i // 4][:, kc, (i % 4) * 128:(i % 4 + 1) * 128], ho[:, kc, :]) for kc in range(KC)],
                   [wod[i // 4], hod], pd_)
                tt(R[:, i, tsl(t)], R[:, i, tsl(t)], p_[:, :T], ALU.add, [Rd[t]] + pd_, [Rd[t]])

        def xattn():
            xattn_prep()
            wqv = dr["w_xq"].rearrange("(kc p) n -> p kc n", p=128)
            wov = dr["w_xo"].rearrange("(kc p) n -> p kc n", p=128)
            wq, wqd, wo, wod = [], [], [], []
            for g in range(2):
                a, b = wload(wqv[:, :, g * 512:(g + 1) * 512])
                wq.append(a)
                wqd.append(b)
            for g in range(2):
                a, b = wload(wov[:, :, g * 512:(g + 1) * 512])
                wo.append(a)
                wod.append(b)
            for t in range(NT):
                xattn_tile(t, wq, wqd, wo, wod)
                if t > 0:
                    layer_norm_resid(3, t - 1)
            layer_norm_resid(3, NT - 1)

        def final_ln(t):
            src = lambda c: R[:, c, tsl(t)]
            pm, pmd, rs, rsd = ln_stats(src, [Rd[t]], T)
            g0, b0 = PV["ln4_g"], PV["ln4_b"]
            outs = [(AF.Identity, lambda c: R[:, c, tsl(t)], lambda c: pv[:, g0 + c:g0 + c + 1],
                     lambda c: pv[:, b0 + c:b0 + c + 1], [Rd[t]])]
            ln_apply(src, [Rd[t]], T, pm, pmd, rs, rsd, outs)

        ph = phases if phases is not None else ["ffn1", "mixer", "xattn", "ffn2"]
        if "ffn1" in ph:
            ffn("ffn1_wg", "ffn1_wu", "ffn1_wd", after_tile=lambda t: layer_norm_resid(1, t))
        if "mixer" in ph:
            mixer()
        if "xattn" in ph:
            xattn()
        if "ffn2" in ph:
            def fin(t):
                final_ln(t)
                out_tile(t)
            ffn("ffn2_wg", "ffn2_wu", "ffn2_wd", after_tile=fin)
        else:
            dump_R_as_out()
        P.emit()
    return nc


def make_consts():
    c = np.zeros((128, NCST, 128), np.float32)
    i = np.arange(128)
    c[:, CI["ident"], :] = np.eye(128)
    c[:, CI["triU"], :] = (i[:, None] <= i[None, :])
    c[:, CI["negUs"], :] = np.where(i[None, :] > i[:, None], 0.0, -30000.0)
    c[:, CI["posLs"], :] = np.where(i[:, None] > i[None, :], 0.0, 30000.0)
    c[:, CI["negUi"], :] = np.where(i[None, :] >= i[:, None], 0.0, -30000.0)
    c[:, CI["onesm"], :] = 1.0 / 1024.0
    c[:, CI["ones"], :] = 1.0
    c[127, CI["sel127"], :] = 1.0
    return np.ascontiguousarray(c.reshape(128, NCST * 128))


def make_pvec(inp):
    pv = np.zeros((128, NPV), np.float32)

    def put(name, arr):
        a = np.asarray(arr, np.float32).reshape(-1, 128).T
        pv[:, PV[name]:PV[name] + a.shape[1]] = a
    for li in range(1, 5):
        put("ln%d_g" % li, inp["ln%d_g" % li][0])
        put("ln%d_b" % li, inp["ln%d_b" % li][0])
    put("cln_g", inp["conv_ln_g"][0])
    put("cln_b", inp["conv_ln_b"][0])
    put("cdw_b", inp["conv_dw_b"][0])
    put("bco", inp["b_conv_out"][0])
    w = np.asarray(inp["conv_dw_w"][0], np.float32)
    pv[:, PV["cdw_w"]:PV["cdw_w"] + 248] = w.reshape(31, 8, 128).transpose(2, 1, 0).reshape(128, 248)
    w = np.asarray(inp["gdn_conv_qkv"][0], np.float32)
    pv[:, PV["gcq"]:PV["gcq"] + 96] = w.reshape(4, 24, 128).transpose(2, 1, 0).reshape(128, 96)
    pv[:, PV["gng"]] = np.asarray(inp["gdn_norm_g"][0], np.float32)
    pv[:, PV["alog"]:PV["alog"] + 8] = np.asarray(inp["gdn_a_log"][0], np.float32)[None, :]
    pv[:, PV["dtb"]:PV["dtb"] + 8] = np.asarray(inp["gdn_dt_bias"][0], np.float32)[None, :]
    return pv


def core_inputs(inp, b, L):
    m = {"xT": np.ascontiguousarray(np.asarray(inp["x"][b, :L], np.float32).T),
         "memT": np.ascontiguousarray(np.asarray(inp["mem"][b], np.float32).T),
         "pvec": make_pvec(inp), "cst": make_consts()}
    for n in WNAMES:
        m[n] = np.ascontiguousarray(np.asarray(inp[n][0], np.float32))
    return m


def kernel(**inputs):
    B, L = inputs["x"].shape[0], inputs["x"].shape[1]
    nc = build(L)
    shared = core_inputs(inputs, 0, L)
    in_maps = []
    for b in range(B):
        m = dict(shared)
        m["xT"] = np.ascontiguousarray(np.asarray(inputs["x"][b], np.float32).T)
        m["memT"] = np.ascontiguousarray(np.asarray(inputs["mem"][b], np.float32).T)
        in_maps.append(m)
    res = run_bass_kernel_spmd(nc, in_maps, core_ids=list(range(B)))
    out = np.stack([np.asarray(r["yT"]).T for r in res.results], axis=0)
    return np.ascontiguousarray(out.astype(np.float32))
```

```python
import math
import numpy as np
import concourse.bass as bass
import concourse.mybir as mybir
from concourse.bass_utils import run_bass_kernel_spmd
from contextlib import ExitStack

F32 = mybir.dt.float32
BF16 = mybir.dt.bfloat16
F32R = mybir.dt.float32r
AF = mybir.ActivationFunctionType
ALU = mybir.AluOpType

ENGS = ("pe", "act", "dve", "pool", "sp")
D = 1024
KC = 8
DFF = 2816
NMEM = 256
ALPHA = 2.0 ** 0.25
LN_EPS = 1e-5
RMS_EPS = 1e-6
NS = 4
DEBUG_TAGS = None

PV = {}
_o = 0
for _n, _w in [("ln1_g", 8), ("ln1_b", 8), ("ln2_g", 8), ("ln2_b", 8), ("ln3_g", 8), ("ln3_b", 8), ("ln4_g", 8),
               ("ln4_b", 8), ("cln_g", 8), ("cln_b", 8), ("cdw_b", 8), ("bco", 8), ("cdw_w", 248), ("gcq", 96),
               ("gng", 1), ("alog", 8), ("dtb", 8)]:
    PV[_n] = _o
    _o += _w
NPV = _o
CI = {"ident": 0, "triU": 1, "negUs": 2, "posLs": 3, "negUi": 4, "onesm": 5, "ones": 6}
NCST = 7


class Dep:
    __slots__ = ("w", "r", "bank")

    def __init__(self, bank=None):
        self.w = None
        self.r = []
        self.bank = bank


class DSem:
    def __init__(self, sem):
        self.sem = sem
        self.cnt = 0


class Prog:
    def __init__(self, nc, es):
        self.nc = nc
        self.es = es
        self.ops = {e: [] for e in ENGS}
        self.sem = {e: es.enter_context(nc.semaphore("s_" + e)) for e in ENGS}
        self.cnt = {e: 0 for e in ENGS}
        self.waited = {e: {} for e in ENGS}
        self.final = []
        self.nsem = 0

    def dsem(self):
        self.nsem += 1
        return DSem(self.es.enter_context(self.nc.semaphore("d%d" % self.nsem)))

    def sb(self, name, shape, dt):
        return self.es.enter_context(self.nc.sbuf_tensor(name, list(shape), dt))

    def ps(self, name, shape, dt):
        return self.es.enter_context(self.nc.psum_tensor(name, list(shape), dt))

    def _need(self, eng, tok, waits):
        if tok is None:
            return
        sem, val, teng = tok
        if teng == eng and eng == "pe":
            return
        k = id(sem)
        if self.waited[eng].get(k, 0) >= val:
            return
        self.waited[eng][k] = val
        waits.append((sem, val))

    def op(self, eng, fn, reads=(), writes=(), dsem=None):
        waits = []
        for d in reads:
            self._need(eng, d.w, waits)
            if d.bank is not None:
                self._need(eng, d.bank[0], waits)
        for d in writes:
            self._need(eng, d.w, waits)
            for t in d.r:
                self._need(eng, t, waits)
        if dsem is None:
            self.cnt[eng] += 1
            tok = (self.sem[eng], self.cnt[eng], eng)
            inc = 1
        else:
            if dsem.cnt > 0:
                self._need(eng, (dsem.sem, dsem.cnt, "dma"), waits)
            dsem.cnt += 16
            tok = (dsem.sem, dsem.cnt, "dma")
            inc = 16
        import sys as _sys
        _f = _sys._getframe(1)
        _tag = []
        for _ in range(3):
            if _f is None:
                break
            _tag.append(_f.f_lineno)
            _f = _f.f_back
        self.ops[eng].append((waits, fn, tok[0], inc, _tag))
        for d in writes:
            d.w = tok
            d.r = []
            if d.bank is not None and eng == "pe":
                d.bank[0] = tok
        for d in reads:
            d.r = [t for t in d.r if t[2] != tok[2] or t[2] == "dma"] + [tok]
        return tok

    def emit(self):
        nc = self.nc
        engmap = {"pe": "tensor", "act": "scalar", "dve": "vector", "pool": "gpsimd", "sp": "sync"}
        fwaits = []
        for d in self.final:
            self._need("sp", d.w, fwaits)
        waited = {}
        for e in ENGS:
            for waits, fn, sem, inc, tag in self.ops[e]:
                for (s_, v_) in waits:
                    waited.setdefault(id(s_), set()).add(v_)
        for (s_, v_) in fwaits:
            waited.setdefault(id(s_), set()).add(v_)
        rank = {}
        for e in ENGS:
            k = id(self.sem[e])
            ms = sorted(waited.get(k, ()))
            rank[k] = {v: i + 1 for i, v in enumerate(ms)}

        def wv(s_, v_):
            r_ = rank.get(id(s_))
            return r_[v_] if r_ is not None else v_

        with nc.Block() as block:
            for e in ENGS:
                ops = self.ops[e]
                extra = fwaits if e == "sp" else []
                if not ops and not extra:
                    continue
                own = self.sem[e]

                def body(eng, ops=ops, extra=extra, own=own):
                    cnt = 0
                    for waits, fn, sem, inc, tag in ops:
                        for (s_, v_) in waits:
                            eng.wait_ge(s_, wv(s_, v_))
                        inst = fn(eng)
                        if sem is own:
                            cnt += 1
                            if cnt in rank[id(own)]:
                                inst.then_inc(sem, 1)
                        else:
                            inst.then_inc(sem, inc)
                    for (s_, v_) in extra:
                        eng.wait_ge(s_, wv(s_, v_))

                getattr(block, engmap[e])(body)


def inherit(new, olds):
    toks = []
    for o in olds:
        if o.w is not None:
            toks.append(o.w)
        toks += o.r
    new.r = new.r + toks


WNAMES = ["ffn1_wg", "ffn1_wu", "ffn1_wd", "w_in", "w_gdn_out", "w_conv_out", "w_mix_out", "w_xq", "w_xkv", "w_xo",
          "ffn2_wg", "ffn2_wu", "ffn2_wd"]
WSHAPES = {"ffn1_wg": (D, DFF), "ffn1_wu": (D, DFF), "ffn1_wd": (DFF, D), "w_in": (D, 8208), "w_gdn_out": (D, D),
           "w_conv_out": (D, D), "w_mix_out": (D, D), "w_xq": (D, D), "w_xkv": (D, 2 * D), "w_xo": (D, D),
           "ffn2_wg": (D, DFF), "ffn2_wu": (D, DFF), "ffn2_wd": (DFF, D)}


def build(L, phases=None):
    nc = bass.Bass("TRN2", target_bir_lowering=False)
    T = min(512, L // 2)
    HL = L // 2
    NT = L // T
    NTH = HL // T
    NPH = HL // 128

    dr = {}
    dr["xT"] = nc.dram_tensor("xT", [D, L], F32, kind="ExternalInput").ap()
    dr["memT"] = nc.dram_tensor("memT", [D, NMEM], F32, kind="ExternalInput").ap()
    for n in WNAMES:
        dr[n] = nc.dram_tensor(n, list(WSHAPES[n]), F32, kind="ExternalInput").ap()
    dr["pvec"] = nc.dram_tensor("pvec", [128, NPV], F32, kind="ExternalInput").ap()
    dr["cst"] = nc.dram_tensor("cst", [128, NCST * 128], F32, kind="ExternalInput").ap()
    yT = nc.dram_tensor("yT", [D, L], F32, kind="ExternalOutput").ap()
    Rsp = nc.dram_tensor("Rsp", [128, 8, L], F32, kind="Internal").ap()
    YCsp = nc.dram_tensor("YCsp", [128, 8, L], F32, kind="Internal").ap()

    def fm(ap):
        return ap.rearrange("(c p) n -> p c n", p=128)

    with ExitStack() as es:
        P = Prog(nc, es)
        Rbuf = P.sb("Rbuf", [128, 8 * L], F32)
        R = Rbuf[:].rearrange("p (c l) -> p c l", c=8)
        xb = P.sb("xb", [128, 8, L], BF16)
        Cb = P.sb("Cb", [128, 8, 32 + L], BF16)
        wslots = [P.sb("wsl%d" % i, [128, 4096], BF16) for i in range(NS)]
        wdeps = [Dep() for _ in range(NS)]
        wsems = [P.dsem() for _ in range(NS)]
        pv = P.sb("pv", [128, NPV], F32)
        cs = P.sb("cs", [128, NCST * 128], F32)
        identb = P.sb("identb", [128, 128], BF16)
        pvx = P.sb("pvx", [128, 64], F32)
        NTF = 6
        tfs = [P.sb("tf%d" % i, [128, 512], F32) for i in range(NTF)]
        tfd = [Dep() for _ in range(NTF)]
        tls = [P.sb("tl%d" % i, [128, 512], F32) for i in range(1)]
        tld = [Dep() for _ in range(1)]
        BIG = L >= 2048
        Cflat = Cb[:].rearrange("p c l -> p (c l)")
        if BIG:
            hbufs = [Cflat[:, i * 8 * T:(i + 1) * 8 * T].rearrange("p (c l) -> p c l", c=8) for i in range(2)]
        else:
            hbufs = [P.sb("hb%d" % i, [128, 8, T], BF16)[:] for i in range(2)]
        hdeps = [Dep() for _ in range(2)]
        psb = [P.ps("psb%d" % i, [128, 512], F32) for i in range(8)]
        psd = []
        for _b in range(8):
            _bi = [None]
            psd.append([Dep(_bi) for _ in range(4)])
        cstd = Dep()
        pvd = Dep()
        state = {"wi": 0, "pa": 0, "tf": 0, "pq": 0, "hb": 0, "tl": 0}

        def tl():
            return tls[0], tld[0]

        def C(name):
            i = CI[name]
            return cs[:, i * 128:(i + 1) * 128]

        def pcol(name, j=0, n=1):
            o = PV[name] + j
            return pv[:, o:o + n]

        def tsl(t):
            return slice(t * T, (t + 1) * T)

        def pa():
            i = state["pa"] % 8
            state["pa"] += 1
            return psb[i], psd[i]

        def pq():
            p_, pd_ = pa()
            return p_[:, 0:128], pd_

        def tf():
            i = state["tf"] % NTF
            state["tf"] += 1
            return tfs[i], tfd[i]

        def wload(view3):
            a, b = view3.shape[1], view3.shape[2]
            i = state["wi"] % NS
            state["wi"] += 1
            dst = wslots[i][:, 0:a * b].rearrange("p (a b) -> p a b", a=a)
            P.op("pool", lambda e: e.dma_start(out=dst, in_=view3), writes=[wdeps[i]], dsem=wsems[i])
            return dst, wdeps[i]

        def mm(out, pairs, reads, pdeps, eng="pe"):
            n = len(pairs)

            def fn(e):
                ins = None
                for i, (l, r) in enumerate(pairs):
                    ins = e.matmul(out, lhsT=l, rhs=r, start=(i == 0), stop=(i == n - 1))
                return ins
            P.op("pe", fn, reads=reads, writes=pdeps)

        def act(out, in_, func, reads, writes, **kw):
            P.op("act", lambda e: e.activation(out=out, in_=in_, func=func, **kw), reads=reads, writes=writes)

        def tt(out, in0, in1, op, reads, writes):
            P.op("dve", lambda e: e.tensor_tensor(out=out, in0=in0, in1=in1, op=op), reads=reads, writes=writes)

        def stt(out, in0, scalar, in1, op0, op1, reads, writes):
            P.op("dve", lambda e: e.scalar_tensor_tensor(out=out, in0=in0, scalar=scalar, in1=in1, op0=op0, op1=op1),
                 reads=reads, writes=writes)

        def ts(out, in0, s1, s2, op0, op1, reads, writes):
            if s2 is None:
                P.op("dve", lambda e: e.tensor_scalar(out=out, in0=in0, scalar1=s1, scalar2=None, op0=op0),
                     reads=reads, writes=writes)
            else:
                P.op("dve", lambda e: e.tensor_scalar(out=out, in0=in0, scalar1=s1, scalar2=s2, op0=op0, op1=op1),
                     reads=reads, writes=writes)

        def cp(out, in_, reads, writes):
            P.op("dve", lambda e: e.tensor_copy(out=out, in_=in_), reads=reads, writes=writes)

        s_c = P.dsem()
        s_p = P.dsem()
        P.op("sp", lambda e: e.dma_start(out=cs[:], in_=dr["cst"]), writes=[cstd], dsem=s_c)
        P.op("sp", lambda e: e.dma_start(out=pv[:], in_=dr["pvec"]), writes=[pvd], dsem=s_p)
        identd = Dep()
        cp(identb[:], C("ident"), [cstd], [identd])
        pvxd = Dep()
        for li in range(3):
            o = PV["ln%d_g" % (li + 1)]
            ts(pvx[:, li * 16:li * 16 + 16], pv[:, o:o + 16], ALPHA, None, ALU.mult, None, [pvd], [pvxd])
        act(pvx[:, 48:56], pcol("alog", 0, 8), AF.Exp, [pvd], [pvxd])
        ts(pvx[:, 48:56], pvx[:, 48:56], -1.0, None, ALU.mult, None, [pvxd], [pvxd])
        ident = C("ident")

        Rd = [Dep() for _ in range(NT)]
        xbd = [Dep() for _ in range(NT)]
        lds = [P.dsem() for _ in range(2)]
        xv = fm(dr["xT"])
        for t in range(NT):
            P.op("sp", lambda e, t=t: e.dma_start(out=R[:, :, tsl(t)], in_=xv[:, :, tsl(t)]), writes=[Rd[t]],
                 dsem=lds[t % 2])
            cp(xb[:, :, tsl(t)], R[:, :, tsl(t)], [Rd[t]], [xbd[t]])
            act(R[:, :, tsl(t)], R[:, :, tsl(t)], AF.Copy, [Rd[t]], [Rd[t]], scale=ALPHA)

        def ffn(wg, wu, wd, after_tile=None):
            wgv = dr[wg].rearrange("(kc p) n -> p kc n", p=128)
            wuv = dr[wu].rearrange("(kc p) n -> p kc n", p=128)
            wdv = dr[wd].rearrange("(j p) n -> p j n", p=128)
            for c0 in range(0, DFF, 512):
                gw = min(512, DFF - c0)
                nj = gw // 128
                wgt, wgd = wload(wgv[:, :, c0:c0 + gw])
                wut, wud = wload(wuv[:, :, c0:c0 + gw])
                wdt, wdd = wload(wdv[:, c0 // 128:c0 // 128 + nj, :])
                for t in range(NT):
                    hi = state["hb"] % 2
                    state["hb"] += 1
                    hb, hd = hbufs[hi], hdeps[hi]
                    for j in range(nj):
                        pg, pgd = pa()
                        mm(pg[:, :T], [(wgt[:, kc, j * 128:(j + 1) * 128], xb[:, kc, tsl(t)]) for kc in range(KC)],
                           [wgd, xbd[t]], pgd)
                        pu, pud = pa()
                        mm(pu[:, :T], [(wut[:, kc, j * 128:(j + 1) * 128], xb[:, kc, tsl(t)]) for kc in range(KC)],
                           [wud, xbd[t]], pud)
                        sg, sgd = tf()
                        act(sg[:, :T], pg[:, :T], AF.Silu, pgd, [sgd])
                        stt(hb[:, j, :], sg[:, :T], 0.5, pu[:, :T], ALU.mult, ALU.mult, [sgd] + pud, [hd])
                    for i in range(8):
                        pd_, pdd = pa()
                        mm(pd_[:, :T], [(wdt[:, j, i * 128:(i + 1) * 128], hb[:, j, :]) for j in range(nj)],
                           [wdd, hd], pdd)
                        tt(R[:, i, tsl(t)], R[:, i, tsl(t)], pd_[:, :T], ALU.add, [Rd[t]] + pdd, [Rd[t]])
                    if after_tile is not None and c0 + 512 >= DFF:
                        after_tile(t)

        def mm1(out, l, r, start, stop, reads, pdeps):
            P.op("pe", lambda e: e.matmul(out, lhsT=l, rhs=r, start=start, stop=stop), reads=reads, writes=pdeps)

        def ln_stats(src, srcdeps, n):
            pm, pmd = pa()
            pe2, pe2d = pa()
            mm(pm[:, :n], [(C("onesm"), src(c)) for c in range(8)], srcdeps + [cstd], pmd)
            for c in range(8):
                sq, sqd = tf()
                act(sq[:, :n], src(c), AF.Square, srcdeps, [sqd])
                mm1(pe2[:, :n], C("onesm"), sq[:, :n], c == 0, c == 7, [sqd, cstd], pe2d)
            msq, msqd = tf()
            act(msq[:, :n], pm[:, :n], AF.Square, pmd, [msqd])
            var, vard = tl()
            tt(var[:, :n], pe2[:, :n], msq[:, :n], ALU.subtract, pe2d + [msqd], [vard])
            act(var[:, :n], var[:, :n], AF.Ln, [vard], [vard], bias=LN_EPS)
            act(var[:, :n], var[:, :n], AF.Exp, [vard], [vard], scale=-0.5)
            return pm, pmd, var, vard

        def ln_apply(src, srcdeps, n, pm, pmd, rs, rsd, outs):
            for c in range(8):
                t1, t1d = tf()
                tt(t1[:, :n], src(c), pm[:, :n], ALU.subtract, srcdeps + pmd, [t1d])
                tt(t1[:, :n], t1[:, :n], rs[:, :n], ALU.mult, [t1d, rsd], [t1d])
                for (func, outf, scf, bif, odeps) in outs:
                    act(outf(c), t1[:, :n], func, [t1d, pvd, pvxd], odeps, scale=scf(c), bias=bif(c))

        def layer_norm_resid(li, t):
            src = lambda c: R[:, c, tsl(t)]
            pm, pmd, rs, rsd = ln_stats(src, [Rd[t]], T)
            g0 = PV["ln%d_g" % li]
            b0 = PV["ln%d_b" % li]
            x0 = (li - 1) * 16
            outs = [
                (AF.Identity, lambda c: xb[:, c, tsl(t)], lambda c: pv[:, g0 + c:g0 + c + 1],
                 lambda c: pv[:, b0 + c:b0 + c + 1], [xbd[t]]),
                (AF.Identity, lambda c: R[:, c, tsl(t)], lambda c: pvx[:, x0 + c:x0 + c + 1],
                 lambda c: pvx[:, x0 + 8 + c:x0 + 9 + c], [Rd[t]]),
            ]
            ln_apply(src, [Rd[t]], T, pm, pmd, rs, rsd, outs)

        dbg = {}
        outd = Dep()
        osem = [P.dsem() for _ in range(2)]
        ocnt = [0]

        outds = [Dep() for _ in range(NT)]

        def out_tile(t):
            k = ocnt[0] % 2
            ocnt[0] += 1
            P.op("sp", lambda e, t=t: e.dma_start(out=fm(yT)[:, :, tsl(t)], in_=R[:, :, tsl(t)]),
                 reads=[Rd[t]], writes=[outds[t]], dsem=osem[k])
            P.final.append(outds[t])

        def dump_R_as_out():
            for t in range(NT):
                out_tile(t)

        def mixer():
            winv = dr["w_in"].rearrange("(kc p) n -> p kc n", p=128)
            tslh = lambda a: slice(a * T, (a + 1) * T)
            spd = [Dep() for _ in range(NT)]
            sps = [P.dsem() for _ in range(2)]
            for t in range(NT):
                P.op("sp", lambda e, t=t: e.dma_start(out=Rsp[:, :, tsl(t)], in_=R[:, :, tsl(t)]), reads=[Rd[t]],
                     writes=[spd[t]], dsem=sps[t % 2])
            Cd = [Dep() for _ in range(NT)]
            Chd = Dep()
            for d in Cd + [Chd]:
                inherit(d, hdeps)
            P.op("dve", lambda e: e.memset(Cb[:, :, 0:32], 0.0), writes=[Chd])
            wls = [wload(winv[:, :, 4112 + g * 512:4112 + (g + 1) * 512]) for g in range(2)]
            wgs = [wload(winv[:, :, 5136 + g * 512:5136 + (g + 1) * 512]) for g in range(2)]
            for t in range(NT):
                for g in range(2):
                    wl, wld = wls[g]
                    wg_, wgd_ = wgs[g]
                    for j in range(4):
                        pl, pld = pa()
                        mm(pl[:, :T], [(wl[:, kc, j * 128:(j + 1) * 128], xb[:, kc, tsl(t)]) for kc in range(KC)],
                           [wld, xbd[t]], pld)
                        pg, pgd = pa()
                        mm(pg[:, :T], [(wg_[:, kc, j * 128:(j + 1) * 128], xb[:, kc, tsl(t)]) for kc in range(KC)],
                           [wgd_, xbd[t]], pgd)
                        sg, sgd = tf()
                        act(sg[:, :T], pg[:, :T], AF.Sigmoid, pgd, [sgd])
                        tt(Cb[:, g * 4 + j, 32 + t * T:32 + (t + 1) * T], sg[:, :T], pl[:, :T], ALU.mult, [sgd] + pld,
                           [Cd[t]])
            C2 = Rbuf[:, 0:8 * HL].rearrange("p (c l) -> p c l", c=8)
            csb = Rbuf[:, 8 * HL:8 * HL + 4 * T].bitcast(BF16).rearrange("p (c l) -> p c l", c=8)
            C2d = [Dep() for _ in range(NTH)]
            csd = Dep()
            for d in C2d + [csd]:
                inherit(d, Rd)
            Dg = P.sb("Dg", [128, 32, 128], BF16)
            Dgd = Dep()
            DgdA, DgdB = Dep(), Dep()
            wcov = dr["w_conv_out"].rearrange("(kc p) n -> p kc n", p=128)
            ycd = [Dep() for _ in range(NT)]
            ycs = [P.dsem() for _ in range(2)]
            for hh in range(2):
                for c in range(8):
                    halves = [(0, 16, DgdA), (16, 31, DgdB)]
                    pss = {}
                    for (k0, k1, dgd_) in halves:
                        for k in range(k0, k1):
                            ts(Dg[:, k, :], identb[:], pcol("cdw_w", c * 31 + k), None, ALU.mult, None, [identd, pvd], [dgd_])
                        for a in range(NTH):
                            t = hh * NTH + a
                            if k0 == 0:
                                pss[a] = pa()
                            p_, pd_ = pss[a]
                            base = 32 + t * T - 30
                            rds = [dgd_, Cd[t], Chd] + ([Cd[t - 1]] if t > 0 else [])
                            n_ = k1 - k0

                            def fn(e, p_=p_, c=c, base=base, k0=k0, k1=k1):
                                ins = None
                                for k in range(k0, k1):
                                    ins = e.matmul(p_[:, :T], lhsT=Dg[:, k, :], rhs=Cb[:, c, base + k:base + k + T],
                                                   start=(k == 0), stop=(k == 30))
                                return ins
                            P.op("pe", fn, reads=rds, writes=pd_)
                    for a in range(NTH):
                        p_, pd_ = pss[a]
                        act(C2[:, c, tslh(a)], p_[:, :T], AF.Identity, pd_ + [pvd], [C2d[a]], bias=pcol("cdw_b", c))
                wco = [wload(wcov[:, :, g * 512:(g + 1) * 512]) for g in range(2)]
                wgc = [wload(winv[:, :, 7184 + g * 512:7184 + (g + 1) * 512]) for g in range(2)]
                for a in range(NTH):
                    t = hh * NTH + a
                    src = lambda c, a=a: C2[:, c, tslh(a)]
                    pm, pmd, rs, rsd = ln_stats(src, [C2d[a]], T)
                    outs = [(AF.Silu, lambda c: csb[:, c, :], lambda c: pcol("cln_g", c), lambda c: pcol("cln_b", c),
                             [csd])]
                    ln_apply(src, [C2d[a]], T, pm, pmd, rs, rsd, outs)
                    for i in range(8):
                        py, pyd = pa()
                        mm(py[:, :T], [(wco[i // 4][0][:, kc, (i % 4) * 128:(i % 4 + 1) * 128], csb[:, kc, :])
                                       for kc in range(KC)], [wco[i // 4][1], csd], pyd)
                        pg, pgd = pa()
                        mm(pg[:, :T], [(wgc[i // 4][0][:, kc, (i % 4) * 128:(i % 4 + 1) * 128], xb[:, kc, tsl(t)])
                                       for kc in range(KC)], [wgc[i // 4][1], xbd[t]], pgd)
                        sg, sgd = tf()
                        act(sg[:, :T], pg[:, :T], AF.Sigmoid, pgd, [sgd])
                        stt(C2[:, i, tslh(a)], py[:, :T], pcol("bco", i), sg[:, :T], ALU.add, ALU.mult,
                            pyd + [sgd, pvd], [C2d[a]])
                    P.op("sp", lambda e, t=t, a=a: e.dma_start(out=YCsp[:, :, tsl(t)], in_=C2[:, :, tslh(a)]),
                         reads=[C2d[a]], writes=[ycd[t]], dsem=ycs[t % 2])

            Rb16 = Rbuf[:].bitcast(BF16)
            QT = Rb16[:, 0:8 * HL].rearrange("p (c l) -> p c l", c=8)
            KT = Rb16[:, 8 * HL:16 * HL].rearrange("p (c l) -> p c l", c=8)
            VT = Rb16[:, 16 * HL:24 * HL].rearrange("p (c l) -> p c l", c=8)
            ZS = Rb16[:, 24 * HL:32 * HL].rearrange("p (c l) -> p c l", c=8)
            qd = [Dep() for _ in range(NTH)]
            kd = [Dep() for _ in range(NTH)]
            vd = [Dep() for _ in range(NTH)]
            zd = [Dep() for _ in range(NTH)]
            for d in qd + kd + vd + zd:
                inherit(d, C2d + [csd] + Rd)
            halo = P.sb("halo", [128, 24, 4], F32)
            halod = Dep()
            P.op("dve", lambda e: e.memset(halo[:], 0.0), writes=[halod])
            Dgflat = Dg[:].rearrange("p a b -> p (a b)")
            stg = [Dgflat[:, i * 1040:i * 1040 + 1032].bitcast(F32) for i in range(2)]
            stgd = [Dep(), Dep()]
            for d in stgd:
                inherit(d, [Dgd, DgdA, DgdB])
            gtm = P.sb("gtm", [128, NPH, 8], F32)
            lbt = P.sb("lbt", [128, NPH, 8], F32)
            bet = P.sb("bet", [128, NPH, 8], F32)
            gcc = P.sb("gcc", [128, NPH, 8], F32)
            glc = P.sb("glc", [128, NPH, 8], F32)
            x8 = P.sb("x8", [128, 16], F32)
            scd, x8d = Dep(), Dep()
            S = P.sb("S", [128, 8, 128], F32)
            Sb = P.sb("Sb", [128, 8, 128], BF16)
            Sd = [Dep() for _ in range(8)]
            P.op("dve", lambda e: e.memset(S[:], 0.0), writes=Sd)
            P.op("dve", lambda e: e.memset(Sb[:], 0.0), reads=[], writes=Sd)
            NSLOT = 4
            NGF, NGB, NSM, NGR = 5, 5, 8, 4
            grb = P.sb("grb", [128, NSLOT * NGR * 128], F32R)
            slot_gr = [[grb[:, (sl_ * NGR + i) * 128:(sl_ * NGR + i + 1) * 128] for i in range(NGR)] for sl_ in range(NSLOT)]
            slot_grd = [[Dep() for _ in range(NGR)] for _ in range(NSLOT)]
            Dgf = Dg[:].rearrange("p a b -> p (a b)")
            slot_gf, slot_gfd, slot_gb, slot_gbd = [], [], [], []
            slot_gf.append([Dgf[:, i * 256:(i + 1) * 256].bitcast(F32) for i in range(NGF)])
            slot_gb.append([Dgf[:, NGF * 256 + i * 128:NGF * 256 + (i + 1) * 128] for i in range(NGB)])
            tfq = []
            for i in (0, 1, 2, 5):
                for q_ in range(4):
                    tfq.append(tfs[i][:, q_ * 128:(q_ + 1) * 128])
            gbhost = [tfs[3], tfs[4], tls[0]]
            for sl_ in range(1, NSLOT):
                slot_gf.append(tfq[(sl_ - 1) * NGF:sl_ * NGF])
                tfb = gbhost[sl_ - 1][:].bitcast(BF16)
                slot_gb.append([tfb[:, i * 128:(i + 1) * 128] for i in range(NGB)])
            for sl_ in range(NSLOT):
                slot_gfd.append([Dep() for _ in range(NGF)])
                slot_gbd.append([Dep() for _ in range(NGB)])
            for d in slot_gfd[0] + slot_gbd[0]:
                inherit(d, [Dgd, DgdA, DgdB])
            gfd = slot_gfd[0]
            gbd = slot_gbd[0]
            sms = P.sb("sms", [128, NSLOT * NSM], F32)
            smd = [Dep() for _ in range(NSLOT * NSM)]
            st2 = {"stg": 0}
            for sl_ in range(NSLOT):
                st2["gf%d" % sl_] = 0
                st2["gb%d" % sl_] = 0
                st2["sm%d" % sl_] = 0
                st2["gr%d" % sl_] = 0

            def grx(sl_):
                i = st2["gr%d" % sl_] % NGR
                st2["gr%d" % sl_] += 1
                return slot_gr[sl_][i], slot_grd[sl_][i]

            def gfx(sl_):
                i = st2["gf%d" % sl_] % NGF
                st2["gf%d" % sl_] += 1
                return slot_gf[sl_][i], slot_gfd[sl_][i]

            def gbx(sl_):
                i = st2["gb%d" % sl_] % NGB
                st2["gb%d" % sl_] += 1
                return slot_gb[sl_][i], slot_gbd[sl_][i]

            def smx(sl_):
                i = st2["sm%d" % sl_] % NSM
                st2["sm%d" % sl_] += 1
                return sms[:, sl_ * NSM + i:sl_ * NSM + i + 1], smd[sl_ * NSM + i]

            seg_dst = [(QT, qd), (KT, kd), (VT, vd), (ZS, zd)]
            for hh in range(2):
                for seg in range(4):
                    dst, dstd = seg_dst[seg]
                    for g in range(2):
                        wt, wtd = wload(winv[:, :, seg * 1024 + g * 512:seg * 1024 + (g + 1) * 512])
                        for j in range(4):
                            ch = g * 4 + j
                            cc = seg * 8 + ch
                            tiles = list(range(NTH))
                            ps_ = {}
                            for a in tiles:
                                t = hh * NTH + a
                                p_, pd_ = pa()
                                mm(p_[:, :T], [(wt[:, kc, j * 128:(j + 1) * 128], xb[:, kc, tsl(t)]) for kc in range(KC)],
                                   [wtd, xbd[t]], pd_)
                                ps_[a] = (p_, pd_)
                            if seg == 3:
                                for a in tiles:
                                    p_, pd_ = ps_[a]
                                    act(dst[:, ch, tslh(a)], p_[:, :T], AF.Silu, pd_, [dstd[a]])
                                continue
                            sts = {}
                            for a in tiles:
                                p_, pd_ = ps_[a]
                                si = st2["stg"] % 2
                                st2["stg"] += 1
                                st, std = stg[si], stgd[si]
                                act(st[:, 3:3 + T], p_[:, :T], AF.Copy, pd_, [std])
                                sts[a] = (st, std)
                            for a in tiles:
                                st, std = sts[a]
                                cp(st[:, 0:3], halo[:, cc, 0:3], [halod], [std])
                                cp(halo[:, cc, 0:3], st[:, T:T + 3], [std], [halod])
                            accs = {}
                            for a in tiles:
                                st, std = sts[a]
                                acc, accd = tf()
                                act(acc[:, :T], st[:, 3:3 + T], AF.Identity, [std, pvd], [accd],
                                    scale=pcol("gcq", cc * 4 + 3), bias=0.0)
                                accs[a] = (acc, accd)
                            for k in range(3):
                                for a in tiles:
                                    st, std = sts[a]
                                    acc, accd = accs[a]
                                    stt(acc[:, :T], st[:, k:k + T], pcol("gcq", cc * 4 + k), acc[:, :T], ALU.mult, ALU.add,
                                        [std, accd, pvd], [accd])
                            if seg == 2:
                                for a in tiles:
                                    acc, accd = accs[a]
                                    act(dst[:, ch, tslh(a)], acc[:, :T], AF.Silu, [accd], [dstd[a]])
                                continue
                            qss, sqq, pns = {}, {}, {}
                            for a in tiles:
                                acc, accd = accs[a]
                                act(acc[:, :T], acc[:, :T], AF.Silu, [accd], [accd])
                            for a in tiles:
                                acc, accd = accs[a]
                                sq, sqd = tf()
                                act(sq[:, :T], acc[:, :T], AF.Square, [accd], [sqd])
                                sqq[a] = (sq, sqd)
                            for a in tiles:
                                sq, sqd = sqq[a]
                                pn, pnd = pa()
                                mm(pn[:, :T], [(C("ones"), sq[:, :T])], [sqd, cstd], pnd)
                                pns[a] = (pn, pnd)
                            for a in tiles:
                                sq, sqd = sqq[a]
                                pn, pnd = pns[a]
                                act(sq[:, :T], pn[:, :T], AF.Ln, pnd, [sqd], bias=RMS_EPS)
                            for a in tiles:
                                sq, sqd = sqq[a]
                                act(sq[:, :T], sq[:, :T], AF.Exp, [sqd], [sqd], scale=-0.5,
                                    bias=(-0.5 * math.log(128.0) if seg == 0 else 0.0))
                            for a in tiles:
                                acc, accd = accs[a]
                                sq, sqd = sqq[a]
                                tt(dst[:, ch, tslh(a)], acc[:, :T], sq[:, :T], ALU.mult, [accd, sqd], [dstd[a]])
                wab, wabd = wload(winv[:, :, 4096:4112])
                for pp in range(NPH):
                    gp = hh * NPH + pp
                    tok = slice(gp * 128, (gp + 1) * 128)
                    t = (gp * 128) // T
                    pab, pabd = pq()
                    mm(pab[:, 0:16], [(xb[:, kc, tok], wab[:, kc, :]) for kc in range(KC)], [xbd[t], wabd], pabd)
                    tt(x8[:, 0:8], pab[:, 0:8], pcol("dtb", 0, 8), ALU.add, pabd + [pvd], [x8d])
                    act(x8[:, 0:8], x8[:, 0:8], AF.Exp, [x8d], [x8d])
                    act(x8[:, 0:8], x8[:, 0:8], AF.Ln, [x8d], [x8d], bias=1.0)
                    tt(gtm[:, pp, :], x8[:, 0:8], pvx[:, 48:56], ALU.mult, [x8d, pvxd], [scd])
                    act(x8[:, 8:16], pab[:, 8:16], AF.Exp, pabd, [x8d], scale=-1.0)
                    act(x8[:, 8:16], x8[:, 8:16], AF.Ln, [x8d], [x8d], bias=1.0)
                    ts(lbt[:, pp, :], x8[:, 8:16], -1.0, None, ALU.mult, None, [x8d], [scd])
                    act(bet[:, pp, :], lbt[:, pp, :], AF.Exp, [scd], [scd])
                    pg_, pgd_ = pq()
                    mm(pg_[:, 0:8], [(C("triU"), gtm[:, pp, :])], [scd, cstd], pgd_)
                    cp(gcc[:, pp, :], pg_[:, 0:8], pgd_, [scd])
                    tt(glc[:, pp, :], gcc[:, pp, :], lbt[:, pp, :], ALU.add, [scd], [scd])
                for sl_ in range(1, NSLOT):
                    for d in slot_gfd[sl_] + slot_gbd[sl_]:
                        inherit(d, tfd + tld)
                for d in slot_gfd[0] + slot_gbd[0]:
                    inherit(d, stgd)

                def head_gen(pp, h, sl_):
                    gp = hh * NPH + pp
                    a = (pp * 128) // T
                    t = (gp * 128) // T
                    hs = slice(pp * 128, (pp + 1) * 128)
                    otok = slice(32 + gp * 128, 32 + (gp + 1) * 128)
                    gf = lambda: gfx(sl_)
                    gr = lambda: grx(sl_)
                    f32v = lambda ap: ap.bitcast(F32)
                    gb = lambda: gbx(sl_)
                    sm = lambda: smx(sl_)
                    kT, qT, vT = KT[:, h, hs], QT[:, h, hs], VT[:, h, hs]
                    Gb_, Gbd_ = gf()
                    cp(Gb_, gtm[:, pp, h:h + 1].to_broadcast([128, 128]), [scd], [Gbd_])
                    Lb_, Lbd_ = gf()
                    cp(Lb_, lbt[:, pp, h:h + 1].to_broadcast([128, 128]), [scd], [Lbd_])
                    yield
                    P2, P2d = pq()
                    mm(P2, [(Gb_, C("triU"))], [Gbd_, cstd], P2d)
                    P1, P1d = pq()
                    mm(P1, [(Gb_, C("triU")), (Lb_, ident)], [Gbd_, Lbd_, cstd], P1d)
                    gcol = gcc[:, pp, h:h + 1]
                    glcol = glc[:, pp, h:h + 1]
                    t1, t1d = gf()
                    stt(t1, P1, gcol, C("negUs"), ALU.subtract, ALU.add, P1d + [scd, cstd], [t1d])
                    act(t1, t1, AF.Exp, [t1d], [t1d])
                    t2, t2d = gf()
                    stt(t2, P2, glcol, C("posLs"), ALU.subtract, ALU.add, P2d + [scd, cstd], [t2d])
                    act(t2, t2, AF.Exp, [t2d], [t2d], scale=-1.0)
                    t3, t3d = gf()
                    stt(t3, P2, gcol, C("negUi"), ALU.subtract, ALU.add, P2d + [scd, cstd], [t3d])
                    act(t3, t3, AF.Exp, [t3d], [t3d])
                    E1, E1d = gf()
                    act(E1, P1, AF.Exp, P1d, [E1d])
                    kbg, kbgd = gb()
                    tt(kbg, kT, E1, ALU.mult, [kd[a], E1d], [kbgd])
                    E2, E2d = gf()
                    act(E2, P2, AF.Exp, P2d, [E2d])
                    qg, qgd = gb()
                    tt(qg, qT, E2, ALU.mult, [qd[a], E2d], [qgd])
                    gl, gld = sm()
                    cp(gl, P2[:, 127:128], P2d, [gld])
                    egl, egld = sm()
                    act(egl, gl, AF.Exp, [gld], [egld])
                    dcol, dcold = sm()
                    act(dcol, gcol, AF.Exp, [scd, gld], [dcold], scale=-1.0, bias=gl)
                    yield
                    Gk, Gkd = pq()
                    mm(Gk, [(kT, kT)], [kd[a]], Gkd)
                    Gq, Gqd = pq()
                    mm(Gq, [(kT, qT)], [kd[a], qd[a]], Gqd)
                    Nn, Nd = gr()
                    tt(Nn, Gk, t1, ALU.mult, Gkd + [t1d], [Nd])
                    Mm, Md = gr()
                    tt(Mm, Gk, t2, ALU.mult, Gkd + [t2d], [Md])
                    AT, ATd = gb()
                    tt(AT, Gq, t3, ALU.mult, Gqd + [t3d], [ATd])
                    Y, Yd = gr()
                    tt(Y, ident, f32v(Nn), ALU.subtract, [cstd, Nd], [Yd])
                    yield
                    Mp, Mpd, Np, Npd = Mm, Md, Nn, Nd
                    for k in range(1, 7):
                        pM, pMd = pq()
                        mm(pM, [(Np, Mp)], [Npd, Mpd], pMd)
                        Mp2, Mp2d = gr()
                        act(Mp2, pM, AF.Copy, pMd, [Mp2d])
                        if k < 6:
                            pN, pNd = pq()
                            mm(pN, [(Mp, Np)], [Npd, Mpd], pNd)
                            Np2, Np2d = gr()
                            cp(Np2, pN, pNd, [Np2d])
                        yield
                        pY, pYd = pq()
                        mm(pY, [(Mp2, Y)], [Mp2d, Yd], pYd)
                        Y2, Y2d = gr()
                        tt(Y2, f32v(Y), pY, ALU.add, [Yd] + pYd, [Y2d])
                        Mp, Mpd, Y, Yd = Mp2, Mp2d, Y2, Y2d
                        if k < 6:
                            Np, Npd = Np2, Np2d
                        yield
                    pk, pkd = pq()
                    pkb = pk.bitcast(BF16)[:, 0:128]
                    P.op("pe", lambda e, pkb=pkb, kT=kT: e.transpose(out=pkb, in_=kT, identity=identb[:]),
                         reads=[kd[a], identd], writes=pkd)
                    kdec, kdecd = gb()
                    act(kdec, pkb, AF.Copy, pkd + [dcold], [kdecd], scale=dcol)
                    pvv, pvvd = pq()
                    pvb = pvv.bitcast(BF16)[:, 0:128]
                    P.op("pe", lambda e, pvb=pvb, vT=vT: e.transpose(out=pvb, in_=vT, identity=identb[:]),
                         reads=[vd[a], identd], writes=pvvd)
                    vbt, vbtd = gf()
                    ts(vbt, pvb, bet[:, pp, h:h + 1], None, ALU.mult, None, pvvd + [scd], [vbtd])
                    yield
                    pR, pRd = pq()
                    mm(pR, [(kbg, Sb[:, h, :])], [kbgd, Sd[h]], pRd)
                    Rs_, Rsd_ = gr()
                    tt(Rs_, vbt, pR, ALU.subtract, [vbtd] + pRd, [Rsd_])
                    yield
                    pvn, pvnd = pq()
                    mm(pvn, [(Y, Rs_)], [Yd, Rsd_], pvnd)
                    vnb, vnbd = gb()
                    act(vnb, pvn, AF.Copy, pvnd, [vnbd])
                    yield
                    po, pod = pq()
                    mm(po, [(qg, Sb[:, h, :]), (AT, vnb)], [qgd, Sd[h], ATd, vnbd], pod)
                    pS, pSd = pq()
                    mm(pS, [(kdec, vnb)], [kdecd, vnbd], pSd)
                    stt(S[:, h, :], S[:, h, :], egl, pS, ALU.mult, ALU.add, [egld] + pSd, [Sd[h]])
                    act(Sb[:, h, :], S[:, h, :], AF.Copy, [Sd[h]], [Sd[h]])
                    junk, junkd = gf()
                    ss, ssd = sm()
                    P.op("act", lambda e, junk=junk, po=po, ss=ss: e.activation(out=junk, in_=po, func=AF.Square,
                                                                                accum_out=ss),
                         reads=pod, writes=[junkd, ssd])
                    ts(ss, ss, 1.0 / 128.0, RMS_EPS, ALU.mult, ALU.add, [ssd], [ssd])
                    act(ss, ss, AF.Ln, [ssd], [ssd])
                    act(ss, ss, AF.Exp, [ssd], [ssd], scale=-0.5)
                    on, ond = gf()
                    act(on, po, AF.Copy, pod + [ssd], [ond], scale=ss)
                    yield
                    pT, pTd = pq()
                    P.op("pe", lambda e, pT=pT, on=on: e.transpose(out=pT, in_=on, identity=ident),
                         reads=[ond, cstd], writes=pTd)
                    stt(Cb[:, h, otok], pT, pcol("gng"), ZS[:, h, hs], ALU.mult, ALU.mult, pTd + [pvd, zd[a]],
                        [Cd[t]])
                    yield

                work = [(pp, h) for pp in range(NPH) for h in range(8)]
                active = {}
                while work or active:
                    for sl_ in range(NSLOT):
                        if sl_ not in active and work:
                            pp_, h_ = work.pop(0)
                            active[sl_] = head_gen(pp_, h_, sl_)
                    for sl_ in list(active):
                        try:
                            next(active[sl_])
                        except StopIteration:
                            del active[sl_]
                for sl_ in range(1, NSLOT):
                    for d in tfd + tld:
                        inherit(d, slot_gfd[sl_] + slot_gbd[sl_])
                for d in stgd:
                    inherit(d, slot_gfd[0] + slot_gbd[0])

            Ytmp = Dg[:].rearrange("p a b -> p (a b)").rearrange("p (c l) -> p c l", c=8)[:, :, 0:T]
            Ytd = Dep()
            inherit(Ytd, [Dgd, DgdA, DgdB] + gfd + gbd)
            wgov = dr["w_gdn_out"].rearrange("(kc p) n -> p kc n", p=128)
            wmov = dr["w_mix_out"].rearrange("(kc p) n -> p kc n", p=128)
            wgo = [wload(wgov[:, :, g * 512:(g + 1) * 512]) for g in range(2)]
            wga = [wload(winv[:, :, 6160 + g * 512:6160 + (g + 1) * 512]) for g in range(2)]
            inherit(Ytd, stgd)
            ycbs = [P.dsem() for _ in range(2)]
            for t in range(NT):
                for i in range(8):
                    pya, pyad = pa()
                    mm(pya[:, :T], [(wgo[i // 4][0][:, kc, (i % 4) * 128:(i % 4 + 1) * 128], Cb[:, kc, 32 + t * T:32 + (t + 1) * T])
                                    for kc in range(KC)], [wgo[i // 4][1], Cd[t]], pyad)
                    pga, pgad = pa()
                    mm(pga[:, :T], [(wga[i // 4][0][:, kc, (i % 4) * 128:(i % 4 + 1) * 128], xb[:, kc, tsl(t)])
                                    for kc in range(KC)], [wga[i // 4][1], xbd[t]], pgad)
                    sg, sgd = tf()
                    act(sg[:, :T], pga[:, :T], AF.Sigmoid, pgad, [sgd])
                    bi = (t * 8 + i) % 2
                    ycb_, ycbd_ = tf()
                    ycb = ycb_[:, 0:T]
                    P.op("sp", lambda e, ycb=ycb, i=i, t=t: e.dma_start(out=ycb, in_=YCsp[:, i, tsl(t)]),
                         reads=[ycd[t]], writes=[ycbd_], dsem=ycbs[bi])
                    tt(sg[:, :T], pya[:, :T], sg[:, :T], ALU.mult, pyad + [sgd], [sgd])
                    tt(Ytmp[:, i, :], sg[:, :T], ycb, ALU.add, [sgd, ycbd_], [Ytd])
                act(Cb[:, :, 32 + t * T:32 + (t + 1) * T], Ytmp, AF.Copy, [Ytd], [Cd[t]])
            wmo = [wload(wmov[:, :, g * 512:(g + 1) * 512]) for g in range(2)]
            rls = [P.dsem() for _ in range(2)]
            for t in range(NT):
                inherit(Rd[t], qd + kd + vd + zd + C2d + [csd])
                P.op("sp", lambda e, t=t: e.dma_start(out=R[:, :, tsl(t)], in_=Rsp[:, :, tsl(t)]), reads=[spd[t]],
                     writes=[Rd[t]], dsem=rls[t % 2])
                for i in range(8):
                    p_, pd_ = pa()
                    mm(p_[:, :T], [(wmo[i // 4][0][:, kc, (i % 4) * 128:(i % 4 + 1) * 128], Cb[:, kc, 32 + t * T:32 + (t + 1) * T])
                                   for kc in range(KC)], [wmo[i // 4][1], Cd[t]], pd_)
                    tt(R[:, i, tsl(t)], R[:, i, tsl(t)], p_[:, :T], ALU.add, [Rd[t]] + pd_, [Rd[t]])
                if t > 0:
                    layer_norm_resid(2, t - 1)
            layer_norm_resid(2, NT - 1)
            for d in hdeps + etd + [memd, KTd, Vd]:
                inherit(d, Cd + [Chd])

        if BIG:
            o0 = 20 * T
            memb = Cflat[:, o0:o0 + 2048].rearrange("p (c l) -> p c l", c=8)
            KTb = Cflat[:, o0 + 2048:o0 + 4096].rearrange("p (c l) -> p c l", c=8)
            Vb = Cflat[:, o0 + 4096:o0 + 6144].rearrange("p (c l) -> p c l", c=2)
        else:
            memb = P.sb("memb", [128, 8, NMEM], BF16)[:]
            KTb = P.sb("KTb", [128, 8, NMEM], BF16)[:]
            Vb = P.sb("Vb", [128, 2, D], BF16)[:]
        onesb = P.sb("onesb", [128, 128], BF16)
        if BIG:
            etb = [Cflat[:, 16 * T + i * 2 * T:16 * T + (i + 1) * 2 * T].rearrange("p (c l) -> p c l", c=2) for i in range(2)]
        else:
            etb = [P.sb("etb%d" % i, [128, 2, T], BF16)[:] for i in range(2)]
        etd = [Dep() for _ in range(2)]
        kms = P.sb("kms", [1, 8], F32)
        memd, KTd, Vd, onesbd, kmsd = Dep(), Dep(), Dep(), Dep(), Dep()

        def xattn_prep():
            s_m = P.dsem()
            P.op("pool", lambda e: e.dma_start(out=memb, in_=fm(dr["memT"])), writes=[memd], dsem=s_m)
            cp(onesb[:], C("ones"), [cstd], [onesbd])
            wv = dr["w_xkv"].rearrange("(kc p) n -> p kc n", p=128)
            for g in range(2):
                wt, wd_ = wload(wv[:, :, g * 512:(g + 1) * 512])
                for j in range(4):
                    p_, pd_ = pa()
                    mm(p_[:, :NMEM], [(wt[:, kc, j * 128:(j + 1) * 128], memb[:, kc, :]) for kc in range(KC)],
                       [wd_, memd], pd_)
                    act(KTb[:, g * 4 + j, :], p_[:, :NMEM], AF.Copy, pd_, [KTd])
            for g in range(2):
                wt, wd_ = wload(wv[:, :, D + g * 512:D + (g + 1) * 512])
                for mc in range(2):
                    p_, pd_ = pa()
                    mm(p_[:, :512], [(memb[:, kc, mc * 128:(mc + 1) * 128], wt[:, kc, :]) for kc in range(KC)],
                       [wd_, memd], pd_)
                    act(Vb[:, mc, g * 512:(g + 1) * 512], p_[:, :512], AF.Copy, pd_, [Vd])
            for h in range(4):
                sqs = []
                for dc in range(2):
                    sq, sqd = tf()
                    act(sq[:, :NMEM], KTb[:, 2 * h + dc, :], AF.Square, [KTd], [sqd])
                    sqs.append((sq, sqd))
                p_, pd_ = pa()
                mm(p_[0:1, :NMEM], [(C("ones")[:, 0:1], sq[:, :NMEM]) for sq, _ in sqs], [d for _, d in sqs] + [cstd], pd_)
                P.op("dve", lambda e, p_=p_, h=h: e.tensor_reduce(out=kms[0:1, h:h + 1], in_=p_[0:1, :NMEM],
                                                                 axis=mybir.AxisListType.X, op=ALU.max),
                     reads=pd_, writes=[kmsd])
            act(kms[0:1, 0:4], kms[0:1, 0:4], AF.Sqrt, [kmsd], [kmsd])
            ts(kms[0:1, 0:4], kms[0:1, 0:4], -1.0, None, ALU.mult, None, [kmsd], [kmsd])

        def xattn_tile(t, wq, wqd, wo, wod):
            hq, hqd = hbufs[0], hdeps[0]
            ho, hod = hbufs[1], hdeps[1]
            for i in range(8):
                p_, pd_ = pa()
                mm(p_[:, :T], [(wq[i // 4][:, kc, (i % 4) * 128:(i % 4 + 1) * 128], xb[:, kc, tsl(t)]) for kc in range(KC)],
                   [wqd[i // 4], xbd[t]], pd_)
                act(hq[:, i, :], p_[:, :T], AF.Copy, pd_, [hqd])
            for h in range(4):
                sqs = []
                for dc in range(2):
                    sq, sqd = tf()
                    act(sq[:, :T], hq[:, 2 * h + dc, :], AF.Square, [hqd], [sqd])
                    sqs.append((sq, sqd))
                p_, pd_ = pa()
                mm(p_[0:1, :T], [(C("ones")[:, 0:1], sq[:, :T]) for sq, _ in sqs], [d for _, d in sqs] + [cstd], pd_)
                rowt_, rowd = tf()
                rowt = rowt_[0:1, :T]
                act(rowt, p_[0:1, :T], AF.Sqrt, pd_, [rowd])
                ts(rowt, rowt, kms[0:1, h:h + 1], None, ALU.mult, None, [rowd, kmsd], [rowd])
                ei = (t * 4 + h) % 2
                et, etdd = etb[ei], etd[ei]
                for mc in range(2):
                    p_, pd_ = pa()
                    msl = slice(mc * 128, (mc + 1) * 128)
                    mm(p_[:, :T], [(KTb[:, 2 * h, msl], hq[:, 2 * h, :]), (KTb[:, 2 * h + 1, msl], hq[:, 2 * h + 1, :]),
                                   (C("ones")[0:1, :], rowt)], [KTd, hqd, rowd, cstd], pd_)
                    act(et[:, mc, :], p_[:, :T], AF.Exp, pd_, [etdd], scale=1.0 / 16.0)
                pden, pdend = pa()
                mm(pden[:, :T], [(onesb[:], et[:, mc, :]) for mc in range(2)], [onesbd, etdd], pdend)
                rd, rdd = tf()
                P.op("dve", lambda e, rd=rd, pden=pden: e.reciprocal(out=rd[:, :T], in_=pden[:, :T]), reads=pdend,
                     writes=[rdd])
                for dc in range(2):
                    p_, pd_ = pa()
                    c0 = (2 * h + dc) * 128
                    mm(p_[:, :T], [(Vb[:, mc, c0:c0 + 128], et[:, mc, :]) for mc in range(2)], [Vd, etdd], pd_)
                    tt(ho[:, 2 * h + dc, :], p_[:, :T], rd[:, :T], ALU.mult, pd_ + [rdd], [hod])
            for i in range(8):
                p_, pd_ = pa()
                mm(p_[:, :T], [(wo[i // 4][:, kc, (i % 4) * 128:(i % 4 + 1) * 128], ho[:, kc, :]) for kc in range(KC)],
                   [wod[i // 4], hod], pd_)
                tt(R[:, i, tsl(t)], R[:, i, tsl(t)], p_[:, :T], ALU.add, [Rd[t]] + pd_, [Rd[t]])

        def xattn():
            xattn_prep()
            wqv = dr["w_xq"].rearrange("(kc p) n -> p kc n", p=128)
            wov = dr["w_xo"].rearrange("(kc p) n -> p kc n", p=128)
            wq, wqd, wo, wod = [], [], [], []
            for g in range(2):
                a, b = wload(wqv[:, :, g * 512:(g + 1) * 512])
                wq.append(a)
                wqd.append(b)
            for g in range(2):
                a, b = wload(wov[:, :, g * 512:(g + 1) * 512])
                wo.append(a)
                wod.append(b)
            for t in range(NT):
                xattn_tile(t, wq, wqd, wo, wod)
                if t > 0:
                    layer_norm_resid(3, t - 1)
            layer_norm_resid(3, NT - 1)

        def final_ln(t):
            src = lambda c: R[:, c, tsl(t)]
            pm, pmd, rs, rsd = ln_stats(src, [Rd[t]], T)
            g0, b0 = PV["ln4_g"], PV["ln4_b"]
            outs = [(AF.Identity, lambda c: R[:, c, tsl(t)], lambda c: pv[:, g0 + c:g0 + c + 1],
                     lambda c: pv[:, b0 + c:b0 + c + 1], [Rd[t]])]
            ln_apply(src, [Rd[t]], T, pm, pmd, rs, rsd, outs)

        ph = phases if phases is not None else ["ffn1", "mixer", "xattn", "ffn2"]
        if "ffn1" in ph:
            ffn("ffn1_wg", "ffn1_wu", "ffn1_wd", after_tile=lambda t: layer_norm_resid(1, t))
        if "mixer" in ph:
            mixer()
        if "xattn" in ph:
            xattn()
        if "ffn2" in ph:
            def fin(t):
                final_ln(t)
                out_tile(t)
            ffn("ffn2_wg", "ffn2_wu", "ffn2_wd", after_tile=fin)
        else:
            dump_R_as_out()
        P.emit()
    return nc


def make_consts():
    c = np.zeros((128, NCST, 128), np.float32)
    i = np.arange(128)
    c[:, CI["ident"], :] = np.eye(128)
    c[:, CI["triU"], :] = (i[:, None] <= i[None, :])
    c[:, CI["negUs"], :] = np.where(i[None, :] > i[:, None], 0.0, -30000.0)
    c[:, CI["posLs"], :] = np.where(i[:, None] > i[None, :], 0.0, 30000.0)
    c[:, CI["negUi"], :] = np.where(i[None, :] >= i[:, None], 0.0, -30000.0)
    c[:, CI["onesm"], :] = 1.0 / 1024.0
    c[:, CI["ones"], :] = 1.0
    return np.ascontiguousarray(c.reshape(128, NCST * 128))


def make_pvec(inp):
    pv = np.zeros((128, NPV), np.float32)

    def put(name, arr):
        a = np.asarray(arr, np.float32).reshape(-1, 128).T
        pv[:, PV[name]:PV[name] + a.shape[1]] = a
    for li in range(1, 5):
        put("ln%d_g" % li, inp["ln%d_g" % li][0])
        put("ln%d_b" % li, inp["ln%d_b" % li][0])
    put("cln_g", inp["conv_ln_g"][0])
    put("cln_b", inp["conv_ln_b"][0])
    put("cdw_b", inp["conv_dw_b"][0])
    put("bco", inp["b_conv_out"][0])
    w = np.asarray(inp["conv_dw_w"][0], np.float32)
    pv[:, PV["cdw_w"]:PV["cdw_w"] + 248] = w.reshape(31, 8, 128).transpose(2, 1, 0).reshape(128, 248)
    w = np.asarray(inp["gdn_conv_qkv"][0], np.float32)
    pv[:, PV["gcq"]:PV["gcq"] + 96] = w.reshape(4, 24, 128).transpose(2, 1, 0).reshape(128, 96)
    pv[:, PV["gng"]] = np.asarray(inp["gdn_norm_g"][0], np.float32)
    pv[:, PV["alog"]:PV["alog"] + 8] = np.asarray(inp["gdn_a_log"][0], np.float32)[None, :]
    pv[:, PV["dtb"]:PV["dtb"] + 8] = np.asarray(inp["gdn_dt_bias"][0], np.float32)[None, :]
    return pv


def core_inputs(inp, b, L):
    m = {"xT": np.ascontiguousarray(np.asarray(inp["x"][b, :L], np.float32).T),
         "memT": np.ascontiguousarray(np.asarray(inp["mem"][b], np.float32).T),
         "pvec": make_pvec(inp), "cst": make_consts()}
    for n in WNAMES:
        m[n] = np.ascontiguousarray(np.asarray(inp[n][0], np.float32))
    return m


def kernel(**inputs):
    B, L = inputs["x"].shape[0], inputs["x"].shape[1]
    nc = build(L)
    shared = core_inputs(inputs, 0, L)
    in_maps = []
    for b in range(B):
        m = dict(shared)
        m["xT"] = np.ascontiguousarray(np.asarray(inputs["x"][b], np.float32).T)
        m["memT"] = np.ascontiguousarray(np.asarray(inputs["mem"][b], np.float32).T)
        in_maps.append(m)
    res = run_bass_kernel_spmd(nc, in_maps, core_ids=list(range(B)))
    out = np.stack([np.asarray(r["yT"]).T for r in res.results], axis=0)
    return np.ascontiguousarray(out.astype(np.float32))
```
